# Optimizing a Trainium2 kernel written in Bass

```python
import jax, jax.numpy as jnp
from jax import lax
import numpy as np

D_MODEL = 2048
BATCH = 8
SEQ = 2048
DEPTH = 1

CHUNK = 64
D_MIX = D_MODEL
D_A = D_MIX // 2
D_B = D_MIX - D_A
GROUP_DIM = 64
N_GROUPS_A = D_A // GROUP_DIM
N_GROUPS_B = D_B // GROUP_DIM
CONV_A_WIDTH = 3
CONV_B_WIDTH = 31
D_IN = 3 * D_A + 2 * D_B
D_FF = 4 * D_MODEL
EPS = 1e-6

kernel_name = "hybrid_shortconv_conformer_conv_block"


def rms_norm(x, gain):
    xf = x.astype(jnp.float32)
    y = xf * lax.rsqrt(jnp.mean(xf * xf, axis=-1, keepdims=True) + EPS)
    return (y * gain.astype(jnp.float32)).astype(x.dtype)


def layer_norm(x, gain, bias):
    xf = x.astype(jnp.float32)
    mu = jnp.mean(xf, axis=-1, keepdims=True)
    var = jnp.mean(jnp.square(xf - mu), axis=-1, keepdims=True)
    y = (xf - mu) * lax.rsqrt(var + EPS)
    return (y * gain.astype(jnp.float32) + bias.astype(jnp.float32)).astype(x.dtype)


def causal_dwconv(u, w):
    k, c = w.shape
    return lax.conv_general_dilated(
        u, w[:, None, :].astype(u.dtype),
        window_strides=(1,),
        padding=[(k - 1, 0)],
        dimension_numbers=("NWC", "WIO", "NWC"),
        feature_group_count=c,
    )


def setup_inputs(seed: int = 0) -> dict:
    key = jax.random.key(seed)
    ks = jax.random.split(key, 16)
    f32 = jnp.float32

    def normal(k, shape, scale):
        return jax.random.normal(k, shape, f32) * scale

    def gain(k, n):
        return jnp.ones((DEPTH, n), f32) + normal(k, (DEPTH, n), 0.02)

    return {
        "x": normal(ks[0], (BATCH, SEQ, D_MODEL), 1.0),
        "mix_pre_gain": gain(ks[1], D_MODEL),
        "w_in": normal(ks[2], (DEPTH, D_MODEL, D_IN), D_MODEL ** -0.5),
        "conv_a_w": normal(ks[3], (DEPTH, CONV_A_WIDTH, D_A), CONV_A_WIDTH ** -0.5),
        "conv_b_w": normal(ks[4], (DEPTH, CONV_B_WIDTH, D_B), CONV_B_WIDTH ** -0.5),
        "conv_b_bias": normal(ks[5], (DEPTH, D_B), 0.02),
        "ln_b_gain": gain(ks[6], D_B),
        "ln_b_bias": normal(ks[7], (DEPTH, D_B), 0.02),
        "w_out": normal(ks[8], (DEPTH, D_MIX, D_MODEL), D_MIX ** -0.5),
        "mix_post_gain": gain(ks[9], D_MODEL),
        "mlp_pre_gain": gain(ks[10], D_MODEL),
        "w_up": normal(ks[11], (DEPTH, D_MODEL, D_FF), D_MODEL ** -0.5),
        "w_down": normal(ks[12], (DEPTH, D_FF, D_MODEL), D_FF ** -0.5),
        "mlp_post_gain": gain(ks[13], D_MODEL),
    }


def reference(x, mix_pre_gain, w_in, conv_a_w, conv_b_w, conv_b_bias, ln_b_gain,
              ln_b_bias, w_out, mix_post_gain, mlp_pre_gain, w_up, w_down,
              mlp_post_gain):
    h = x
    for l in range(DEPTH):
        u = rms_norm(h, mix_pre_gain[l])
        p = jnp.einsum("bsd,de->bse", u, w_in[l])
        b_gate, c_gate, v_a, val_b, gate_b = jnp.split(
            p, [D_A, 2 * D_A, 3 * D_A, 3 * D_A + D_B], axis=-1)

        y_a = b_gate * causal_dwconv(c_gate * v_a, conv_a_w[l])

        g = val_b * jax.nn.sigmoid(gate_b)
        g = causal_dwconv(g, conv_b_w[l]) + conv_b_bias[l].astype(g.dtype)
        y_b = jax.nn.silu(layer_norm(g, ln_b_gain[l], ln_b_bias[l]))

        mixed = jnp.concatenate([y_a, y_b], axis=-1)
        o = jnp.einsum("bse,ed->bsd", mixed, w_out[l])
        h = h + rms_norm(o, mix_post_gain[l])

        m = rms_norm(h, mlp_pre_gain[l])
        z = jnp.square(jax.nn.relu(jnp.einsum("bsd,df->bsf", m, w_up[l])))
        z = jnp.einsum("bsf,fd->bsd", z, w_down[l])
        h = h + rms_norm(z, mlp_post_gain[l])
    return h
```

```python
from contextlib import ExitStack

import numpy as np
import concourse.bass as bass
import concourse.mybir as mybir
from concourse.bass_utils import run_bass_kernel_spmd

F32 = mybir.dt.float32
BF16 = mybir.dt.bfloat16
ALU = mybir.AluOpType
AF = mybir.ActivationFunctionType


class Cfg:
    def __init__(self, D=2048, DA=1024, DB=1024, DFF=8192, S=2048, T=512, KS=16, KA=3, KB=31,
                 NSLOT=8, NXS=3, NSQ=6, NF=6, EPS=1e-6):
        self.D, self.DA, self.DB, self.DFF, self.S, self.T = D, DA, DB, DFF, S, T
        self.KS, self.KA, self.KB, self.NSLOT, self.NXS, self.NSQ, self.NF, self.EPS = KS, KA, KB, NSLOT, NXS, NSQ, NF, EPS
        self.DC, self.AC, self.BC, self.FC = D // 128, DA // 128, DB // 128, DFF // 128
        self.MC = self.AC + self.BC
        self.DIN = 3 * DA + 2 * DB
        self.NT = S // T
        self.FH = self.FC // 2
        self.HB = KB - 1
        assert self.DC % KS == 0 and self.MC % KS == 0 and self.FH % KS == 0
        assert self.FH >= self.MC + 2 * self.BC and self.DC % 4 == 0
        self.NU = self.DIN // 128 * (self.DC // KS) + self.DC * (self.MC // KS) \
            + 2 * (self.FH * (self.DC // KS) + self.DC * (self.FH // KS))
        DC, AC, BC = self.DC, self.AC, self.BC
        self.o_g1, self.o_g2, self.o_g3, self.o_g4 = 0, DC, 2 * DC, 3 * DC
        self.o_wa = 4 * DC
        self.o_wb = self.o_wa + AC * KA
        self.o_bb = self.o_wb + BC * KB
        self.o_lg = self.o_bb + BC
        self.o_lb = self.o_lg + BC
        self.NP = self.o_lb + BC


def _chunkvec(v):
    return np.ascontiguousarray(v.reshape(-1, 128).T)


def pack_params(cfg, g1, g2, g3, g4, wa, wb, bb, lg, lb):
    cols = [_chunkvec(g1), _chunkvec(g2), _chunkvec(g3), _chunkvec(g4),
            wa.T.reshape(cfg.AC, 128, cfg.KA).transpose(1, 0, 2).reshape(128, -1),
            wb.T.reshape(cfg.BC, 128, cfg.KB).transpose(1, 0, 2).reshape(128, -1),
            _chunkvec(bb), _chunkvec(lg), _chunkvec(lb)]
    p = np.ascontiguousarray(np.concatenate(cols, axis=1).astype(np.float32))
    assert p.shape == (128, cfg.NP)
    return p


def pack_weights(cfg, w_in, w_out, w_up, w_down):
    KS = cfg.KS
    wall = np.empty((cfg.NU, 128, KS * 128), np.float32)
    u = 0

    def put(W, k0, nb):
        nonlocal u
        blk = W[k0 * 128:(k0 + KS) * 128, nb * 128:(nb + 1) * 128]
        wall[u] = blk.reshape(KS, 128, 128).transpose(1, 0, 2).reshape(128, KS * 128)
        u += 1

    DA, DB = cfg.DA, cfg.DB
    for m in s1_colblocks(cfg):
        for s in range(cfg.DC // KS):
            put(w_in, s * KS, m)
    for n in range(cfg.DC):
        for s in range(cfg.MC // KS):
            put(w_out, s * KS, n)
    for half in range(2):
        for fb in range(cfg.FH):
            for s in range(cfg.DC // KS):
                put(w_up, s * KS, half * cfg.FH + fb)
        for n in range(cfg.DC):
            for s in range(cfg.FH // KS):
                put(w_down, half * cfg.FH + s * KS, n)
    assert u == cfg.NU
    return wall


def s1_colblocks(cfg):
    AC, BC = cfg.AC, cfg.BC
    out = []
    for c in range(BC):
        out += [3 * AC + c, 3 * AC + BC + c]
    for c in range(AC):
        out += [AC + c, 2 * AC + c, c]
    return out


class EngStream:
    def __init__(self, name):
        self.name, self.ops, self.count = name, [], 0

    def add(self, fn, waits, inc=True):
        if inc:
            self.count += 1
        self.ops.append((tuple(waits), fn, inc, None))
        return (self.name, self.count) if inc else None

    def add_dma(self, fn, waits, sem):
        self.ops.append((tuple(waits), fn, False, sem))


class Sched:
    def __init__(self):
        self.E = {n: EngStream(n) for n in ("pe", "act", "dve", "pool", "sp")}
        self.track = {}
        self.const = set()
        self.dcount = {}

    def deps(self, reads, writes):
        w = []
        for k in reads:
            st = self.track.get(k)
            if st and st[0] is not None:
                w.append(st[0])
        for k in writes:
            st = self.track.get(k)
            if st:
                if st[0] is not None:
                    w.append(st[0])
                w.extend(st[1].items())
        return w

    def commit(self, tok, reads, writes):
        for k in reads:
            if k in self.const:
                continue
            st = self.track.setdefault(k, [None, {}])
            if st[1].get(tok[0], 0) < tok[1]:
                st[1][tok[0]] = tok[1]
        for k in writes:
            self.track[k] = [tok, {}]

    def op(self, eng, fn, reads=(), writes=()):
        tok = self.E[eng].add(fn, self.deps(reads, writes), True)
        self.commit(tok, reads, writes)
        return tok

    def dma(self, eng, fn, sem, reads=(), writes=()):
        n = self.dcount.get(sem, 0) + 1
        self.dcount[sem] = n
        self.E[eng].add_dma(fn, self.deps(reads, writes), sem)
        tok = (sem, 16 * n)
        self.commit(tok, reads, writes)
        return tok

    def mm_group(self, out_ap, bank_key, items):
        pe = self.E["pe"]
        n = len(items)
        allreads = []
        tok = None
        for i, (lhsT, rhs, reads) in enumerate(items):
            waits = self.deps(reads, [bank_key] if i == 0 else [])
            last = i == n - 1
            tok = pe.add(_mm(out_ap, lhsT, rhs, i == 0, last), waits, inc=last)
            allreads += list(reads)
        self.commit(tok, allreads, [bank_key])
        return tok

    def stat_mm(self, out_ap, bank_key, lhsT, rhs, first, last, reads):
        waits = self.deps(reads, [bank_key] if first else [])
        tok = self.E["pe"].add(_mm(out_ap, lhsT, rhs, first, last), waits, inc=True)
        self.commit(tok, reads, [bank_key])
        return tok


def _mm(out_ap, lhsT, rhs, start, stop):
    return lambda h: h.matmul(out_ap, lhsT, rhs, start=start, stop=stop)


def _act(out, in_, func, bias=None, scale=None):
    kw = {}
    if bias is not None:
        kw["bias"] = bias
    if scale is not None:
        kw["scale"] = scale
    return lambda h: h.activation(out=out, in_=in_, func=func, **kw)


def _tt(out, in0, in1, op):
    return lambda h: h.tensor_tensor(out=out, in0=in0, in1=in1, op=op)


def _stt(out, in0, scalar, in1, op0, op1):
    return lambda h: h.scalar_tensor_tensor(out=out, in0=in0, scalar=scalar, in1=in1, op0=op0, op1=op1)


def _ts(out, in0, s1, s2, op0, op1=None):
    if op1 is None:
        return lambda h: h.tensor_scalar(out=out, in0=in0, scalar1=s1, scalar2=None, op0=op0)
    return lambda h: h.tensor_scalar(out=out, in0=in0, scalar1=s1, scalar2=s2, op0=op0, op1=op1)


def _copy(out, in_):
    return lambda h: h.tensor_copy(out, in_)


def _recip(out, in_):
    return lambda h: h.reciprocal(out=out, in_=in_)


def _memset(ap, v):
    return lambda h: h.memset(ap, v)


def _dma(out, in_):
    return lambda h: h.dma_start(out=out, in_=in_)


def build_program(cfg):
    D, S_, T = cfg.D, cfg.S, cfg.T
    DC, AC, BC, MC, FH, KS, KA, KB, HB = cfg.DC, cfg.AC, cfg.BC, cfg.MC, cfg.FH, cfg.KS, cfg.KA, cfg.KB, cfg.HB
    NSLOT, NXS, NSQ, NF, EPS, NT, NP, NU = cfg.NSLOT, cfg.NXS, cfg.NSQ, cfg.NF, cfg.EPS, cfg.NT, cfg.NP, cfg.NU

    nc = bass.Bass("TRN2", target_bir_lowering=False)
    xT = nc.dram_tensor("xT", [D, S_], F32, kind="ExternalInput")
    wall = nc.dram_tensor("wall", [NU, 128, KS * 128], F32, kind="ExternalInput")
    prm = nc.dram_tensor("prm", [128, NP], F32, kind="ExternalInput")
    outT = nc.dram_tensor("outT", [D, S_], F32, kind="ExternalOutput")
    outT_v = outT.rearrange("(c p) t -> p c t", p=128)

    with ExitStack() as es:
        def sb(name, shape, dt):
            return es.enter_context(nc.sbuf_tensor(name, shape, dt))

        P = sb("P", [128, NP], F32)
        onesD = sb("onesD", [128, 128], BF16)
        onesB = sb("onesB", [128, 128], BF16)
        ident = sb("ident", [128, 128], BF16)
        epsT = sb("epsT", [128, 1], F32)
        hbuf = sb("hbuf", [128, DC, T], F32)
        zbuf = sb("zbuf", [128, DC, T], F32)
        ab = sb("ab", [128, DC, T], BF16)
        hid = sb("hid", [128, FH, T], BF16)
        gb = sb("gb", [128, BC, HB + T], BF16)
        dg = sb("dg", [128, 2, KB, 128], BF16)
        xs = sb("xs", [128, NXS, T], F32)
        sq = sb("sq", [128, NSQ, T], BF16)
        wsl = sb("wsl", [128, NSLOT, KS * 128], BF16)
        cvw = sb("cvw", [128, 2, T + KA - 1], F32)
        cvh = sb("cvh", [128, AC, KA - 1], F32)
        ft = sb("ft", [128, NF, T], F32)
        rX = sb("rX", [128, T], F32)
        nmr = sb("nmr", [128, T], F32)
        qe = sb("qe", [128, T], F32)
        ps = [es.enter_context(nc.psum_tensor(f"ps{b}", [128, T], F32)) for b in range(8)]
        hidF = hid.bitcast(F32)

        def G(c):
            return bass.AP(hidF, MC * (T // 2) + c * T, [[FH * (T // 2), 128], [1, T]])

        def Gkeys(c):
            return [("hid", MC + 2 * c), ("hid", MC + 2 * c + 1)]

        def pcol(off):
            return P[:, off:off + 1]

        semnames = ["pe", "act", "dve", "pool", "sp", "prm"] + [f"w{i}" for i in range(NSLOT)] \
            + [f"xs{i}" for i in range(NXS)] + [f"st{i}" for i in range(DC // 4)] + [f"hx{i}" for i in range(DC // 4)]
        sems = {n: es.enter_context(nc.semaphore(n)) for n in semnames}
        block = es.enter_context(nc.Block())

        S = Sched()
        S.const |= {("P",), ("onesD",), ("onesB",), ("ident",), ("eps",)}
        SA, SB = 6, 7
        state = {"bank": 0, "xs": 0, "sq": 0, "ft": 0, "wl": 0, "wu": 0}
        sq_pending = [False] * NSQ

        def bank():
            b = state["bank"]
            state["bank"] = (b + 1) % 6
            return b

        def xs_next():
            k = state["xs"]
            state["xs"] = (k + 1) % NXS
            return k

        def sq_next():
            q = state["sq"]
            state["sq"] = (q + 1) % NSQ
            assert not sq_pending[q], "sq slot reused before its consumer was emitted"
            sq_pending[q] = True
            return q

        def ft_next():
            f = state["ft"]
            state["ft"] = (f + 1) % NF
            return f

        def stat(bk, ones, ones_key, q, first, last):
            S.stat_mm(ps[bk][:, :], ("ps", bk), ones[:, :], sq[:, q, :], first, last, [("sq", q), ones_key])
            sq_pending[q] = False

        total_loads = NU * NT

        def w_issue():
            l = state["wl"]
            if l >= total_loads:
                return
            slot, u = l % NSLOT, l % NU
            S.dma("pool", _dma(wsl[:, slot, :], wall[u, :, :]), f"w{slot}", writes=[("w", slot)])
            state["wl"] = l + 1

        def w_take(n):
            l = state["wu"]
            state["wu"] = l + n
            assert state["wu"] <= state["wl"]
            return [(l + i) % NSLOT for i in range(n)]

        def wgroup(bk, nk, rhs_of, rkey_of):
            slots = w_take(nk // KS)
            items = []
            for kk in range(nk):
                s = slots[kk // KS]
                o = (kk % KS) * 128
                items.append((wsl[:, s, o:o + 128], rhs_of(kk), [("w", s), rkey_of(kk)]))
            tok = S.mm_group(ps[bk][:, :], ("ps", bk), items)
            for _ in slots:
                w_issue()
            return tok

        def ld_x(j, c, k):
            S.dma("sp", _dma(xs[:, k, :], xT[c * 128:(c + 1) * 128, j * T:(j + 1) * T]), f"xs{k}",
                  writes=[("xs", k)])

        xT_v = xT.rearrange("(c p) t -> p c t", p=128)
        S.dma("sp", _dma(P[:, :], prm[:, :]), "prm", writes=[("P",)])
        S.op("dve", _memset(onesD[:, :], 1.0 / D), writes=[("onesD",)])
        S.op("dve", _memset(onesB[:, :], 1.0 / cfg.DB), writes=[("onesB",)])
        S.op("dve", _memset(epsT[:, :], EPS), writes=[("eps",)])
        for c in range(BC):
            S.op("dve", _memset(gb[:, c, 0:HB], 0.0), writes=[("gbh", c)])
        for c in range(AC):
            S.op("dve", _memset(cvh[:, c, :], 0.0), writes=[("cvh", c)])

        for _ in range(NSLOT):
            w_issue()

        def ld_h(j, g):
            S.dma("sp", _dma(hbuf[:, 4 * g:4 * g + 4, :], xT_v[:, 4 * g:4 * g + 4, j * T:(j + 1) * T]), f"hx{g}",
                  writes=[("hb", c) for c in range(4 * g, 4 * g + 4)])

        def prep_a_chunk(j, c):
            k = xs_next()
            ld_x(j, c, k)
            q = sq_next()
            S.op("act", _act(sq[:, q, :], xs[:, k, :], AF.Square), reads=[("xs", k)], writes=[("sq", q)])
            return q

        def prep_a_fin():
            f = ft_next()
            S.op("act", _act(ft[:, f, :], ps[SA][:, :], AF.Sqrt, bias=epsT[:, 0:1]),
                 reads=[("ps", SA), ("eps",)], writes=[("ft", f)])
            S.op("dve", _recip(rX[:, :], ft[:, f, :]), reads=[("ft", f)], writes=[("rX",)])

        def prep_b_chunk(j, c):
            k = xs_next()
            ld_x(j, c, k)
            S.op("dve", _stt(ab[:, c, :], xs[:, k, :], pcol(cfg.o_g1 + c), rX[:, :], ALU.mult, ALU.mult),
                 reads=[("xs", k), ("rX",), ("P",)], writes=[("ab", c)])

        def cdiv(a, b):
            return -(-a // b)

        def s1(j, epi):
            pend = []

            def tick():
                for p in pend:
                    p[0] -= 1
                while pend and pend[0][0] <= 0:
                    pend.pop(0)[1]()

            def flush():
                while pend:
                    tick()

            def conv(c):
                bk = bank()
                b = c % 2
                items = [(dg[:, b, k, :], gb[:, c, k:k + T], [("dg", b), ("gbc", c), ("gbh", c)]) for k in range(KB)]
                S.mm_group(ps[bk][:, :], ("ps", bk), items)
                S.op("dve", _copy(gb[:, c, 0:HB], gb[:, c, T:T + HB]), reads=[("gbc", c)], writes=[("gbh", c)])
                S.op("act", _act(G(c), ps[bk][:, :], AF.Identity, bias=pcol(cfg.o_bb + c)),
                     reads=[("ps", bk), ("P",)], writes=Gkeys(c))
                q1 = sq_next()
                S.op("act", _act(sq[:, q1, :], G(c), AF.Identity), reads=Gkeys(c), writes=[("sq", q1)])
                q2 = sq_next()
                S.op("act", _act(sq[:, q2, :], G(c), AF.Square), reads=Gkeys(c), writes=[("sq", q2)])

                def st():
                    stat(SA, onesB, ("onesB",), q1, c == 0, c == BC - 1)
                    stat(SB, onesB, ("onesB",), q2, c == 0, c == BC - 1)
                pend.append([1, st])

            def lnfin():
                fm = ft_next()
                S.op("act", _act(ft[:, fm, :], ps[SA][:, :], AF.Identity), reads=[("ps", SA)], writes=[("ft", fm)])
                f2 = ft_next()
                S.op("dve", _tt(ft[:, f2, :], ft[:, fm, :], ft[:, fm, :], ALU.mult), reads=[("ft", fm)], writes=[("ft", f2)])
                f3 = ft_next()
                S.op("dve", _tt(ft[:, f3, :], ps[SB][:, :], ft[:, f2, :], ALU.subtract),
                     reads=[("ps", SB), ("ft", f2)], writes=[("ft", f3)])
                f4 = ft_next()
                S.op("act", _act(ft[:, f4, :], ft[:, f3, :], AF.Sqrt, bias=epsT[:, 0:1]),
                     reads=[("ft", f3), ("eps",)], writes=[("ft", f4)])
                S.op("dve", _recip(rX[:, :], ft[:, f4, :]), reads=[("ft", f4)], writes=[("rX",)])
                S.op("dve", _stt(nmr[:, :], ft[:, fm, :], -1.0, rX[:, :], ALU.mult, ALU.mult),
                     reads=[("ft", fm), ("rX",)], writes=[("nmr",)])

            def lnapply(c):
                f = ft_next()
                S.op("dve", _tt(ft[:, f, :], G(c), rX[:, :], ALU.mult), reads=Gkeys(c) + [("rX",)], writes=[("ft", f)])
                S.op("dve", _tt(ft[:, f, :], ft[:, f, :], nmr[:, :], ALU.add), reads=[("ft", f), ("nmr",)], writes=[("ft", f)])
                S.op("act", _act(hid[:, AC + c, :], ft[:, f, :], AF.Silu, bias=pcol(cfg.o_lb + c), scale=pcol(cfg.o_lg + c)),
                     reads=[("ft", f), ("P",)], writes=[("hid", AC + c)])

            for c in range(BC):
                bv, bg = bank(), bank()
                wgroup(bv, DC, lambda kk: ab[:, kk, :], lambda kk: ("ab", kk))
                wgroup(bg, DC, lambda kk: ab[:, kk, :], lambda kk: ("ab", kk))
                f = ft_next()
                S.op("act", _act(ft[:, f, :], ps[bg][:, :], AF.Sigmoid), reads=[("ps", bg)], writes=[("ft", f)])
                S.op("dve", _tt(gb[:, c, HB:HB + T], ps[bv][:, :], ft[:, f, :], ALU.mult),
                     reads=[("ps", bv), ("ft", f)], writes=[("gbc", c)])
                b = c % 2
                wbc = P[:, cfg.o_wb + c * KB: cfg.o_wb + (c + 1) * KB].unsqueeze(2).broadcast_to([128, KB, 128])
                idc = ident[:, :].unsqueeze(1).broadcast_to([128, KB, 128])
                S.op("dve", _tt(dg[:, b, :, :], idc, wbc, ALU.mult), reads=[("ident",), ("P",)], writes=[("dg", b)])
                pend.append([2, (lambda c=c: conv(c))])
                tick()

            ln = {"fin": False, "next": 0}
            for i in range(AC):
                bC, bV, bB = bank(), bank(), bank()
                wgroup(bC, DC, lambda kk: ab[:, kk, :], lambda kk: ("ab", kk))
                wgroup(bV, DC, lambda kk: ab[:, kk, :], lambda kk: ("ab", kk))
                wgroup(bB, DC, lambda kk: ab[:, kk, :], lambda kk: ("ab", kk))
                c = i
                f1 = ft_next()
                S.op("act", _act(ft[:, f1, :], ps[bC][:, :], AF.Identity), reads=[("ps", bC)], writes=[("ft", f1)])
                w = c % 2
                S.op("dve", _copy(cvw[:, w, 0:KA - 1], cvh[:, c, :]), reads=[("cvh", c)], writes=[("cvw", w)])
                S.op("dve", _tt(cvw[:, w, KA - 1:KA - 1 + T], ps[bV][:, :], ft[:, f1, :], ALU.mult),
                     reads=[("ps", bV), ("ft", f1)], writes=[("cvw", w)])
                f2 = ft_next()
                S.op("dve", _ts(ft[:, f2, :], cvw[:, w, KA - 1:KA - 1 + T], pcol(cfg.o_wa + c * KA + KA - 1), None, ALU.mult),
                     reads=[("cvw", w), ("P",)], writes=[("ft", f2)])
                for k in range(KA - 2, -1, -1):
                    S.op("dve", _stt(ft[:, f2, :], cvw[:, w, k:k + T], pcol(cfg.o_wa + c * KA + k), ft[:, f2, :], ALU.mult, ALU.add),
                         reads=[("cvw", w), ("ft", f2), ("P",)], writes=[("ft", f2)])
                S.op("dve", _tt(hid[:, c, :], ps[bB][:, :], ft[:, f2, :], ALU.mult),
                     reads=[("ps", bB), ("ft", f2)], writes=[("hid", c)])
                S.op("dve", _copy(cvh[:, c, :], cvw[:, w, T:T + KA - 1]), reads=[("cvw", w)], writes=[("cvh", c)])
                tick()
                for _ in range(cdiv(len(epi), AC - i)):
                    epi.pop(0)()
                if not ln["fin"]:
                    if not pend:
                        lnfin()
                        ln["fin"] = True
                else:
                    for _ in range(cdiv(BC - ln["next"], max(1, AC - 2 - i))):
                        if ln["next"] < BC:
                            lnapply(ln["next"])
                            ln["next"] += 1
            flush()
            while epi:
                epi.pop(0)()
            if not ln["fin"]:
                lnfin()
            while ln["next"] < BC:
                lnapply(ln["next"])
                ln["next"] += 1

        def s2(j):
            pend = None
            for n in range(DC):
                bk = bank()
                wgroup(bk, MC, lambda kk: hid[:, kk, :], lambda kk: ("hid", kk))
                if pend is not None:
                    pend()
                S.op("act", _act(zbuf[:, n, :], ps[bk][:, :], AF.Identity, scale=pcol(cfg.o_g2 + n)),
                     reads=[("ps", bk), ("P",)], writes=[("zb", n)])
                q = sq_next()
                S.op("act", _act(sq[:, q, :], ps[bk][:, :], AF.Square), reads=[("ps", bk)], writes=[("sq", q)])
                pend = (lambda q=q, n=n: stat(SA, onesD, ("onesD",), q, n == 0, n == DC - 1))
            pend()
            f = ft_next()
            S.op("act", _act(ft[:, f, :], ps[SA][:, :], AF.Sqrt, bias=epsT[:, 0:1]),
                 reads=[("ps", SA), ("eps",)], writes=[("ft", f)])
            S.op("dve", _recip(rX[:, :], ft[:, f, :]), reads=[("ft", f)], writes=[("rX",)])
            for c in range(DC):
                e = "pool" if c % 3 == 2 else "dve"
                S.op(e, _tt(zbuf[:, c, :], zbuf[:, c, :], rX[:, :], ALU.mult),
                     reads=[("zb", c), ("rX",)], writes=[("zb", c)])
                S.op(e, _tt(hbuf[:, c, :], zbuf[:, c, :], hbuf[:, c, :], ALU.add),
                     reads=[("zb", c), ("hb", c)], writes=[("hb", c)])
                S.op("act", _act(ab[:, c, :], hbuf[:, c, :], AF.Identity, scale=pcol(cfg.o_g3 + c)),
                     reads=[("hb", c), ("P",)], writes=[("ab", c)])

        def s34(j):
            nxt = j + 1 < NT
            for half in range(2):
                for fb in range(FH):
                    qh = None
                    if half == 0 and fb < DC:
                        qh = sq_next()
                        S.op("act", _act(sq[:, qh, :], hbuf[:, fb, :], AF.Square), reads=[("hb", fb)], writes=[("sq", qh)])
                    bk = bank()
                    wgroup(bk, DC, lambda kk: ab[:, kk, :], lambda kk: ("ab", kk))
                    if qh is not None:
                        stat(SB, onesD, ("onesD",), qh, fb == 0, fb == DC - 1)
                        if fb == DC - 1:
                            S.op("act", _act(qe[:, :], ps[SB][:, :], AF.Square, bias=epsT[:, 0:1]),
                                 reads=[("ps", SB), ("eps",)], writes=[("qe",)])
                    f = ft_next()
                    S.op("act", _act(ft[:, f, :], ps[bk][:, :], AF.Relu), reads=[("ps", bk)], writes=[("ft", f)])
                    S.op("dve", _tt(hid[:, fb, :], ft[:, f, :], ft[:, f, :], ALU.mult), reads=[("ft", f)], writes=[("hid", fb)])
                pend = None
                for n in range(DC):
                    qx = None
                    if nxt and half == 0:
                        qx = prep_a_chunk(j + 1, n)
                    if nxt and half == 1:
                        prep_b_chunk(j + 1, n)
                    bk = bank()
                    wgroup(bk, FH, lambda kk: hid[:, kk, :], lambda kk: ("hid", kk))
                    if qx is not None:
                        stat(SA, onesD, ("onesD",), qx, n == 0, n == DC - 1)
                    if pend is not None:
                        pend()
                        pend = None
                    if half == 0:
                        S.op("act", _act(zbuf[:, n, :], ps[bk][:, :], AF.Identity), reads=[("ps", bk)], writes=[("zb", n)])
                    else:
                        S.op("dve", _tt(zbuf[:, n, :], ps[bk][:, :], zbuf[:, n, :], ALU.add),
                             reads=[("ps", bk), ("zb", n)], writes=[("zb", n)])
                        q = sq_next()
                        S.op("act", _act(sq[:, q, :], zbuf[:, n, :], AF.Square), reads=[("zb", n)], writes=[("sq", q)])
                        pend = (lambda q=q, n=n: stat(SB, onesD, ("onesD",), q, n == 0, n == DC - 1))
                if pend is not None:
                    pend()
                if half == 0 and nxt:
                    prep_a_fin()

        def epilogue_head(j):
            f = ft_next()
            S.op("dve", _stt(ft[:, f, :], qe[:, :], EPS, ps[SB][:, :], ALU.mult, ALU.add),
                 reads=[("qe",), ("ps", SB)], writes=[("ft", f)])
            f2 = ft_next()
            S.op("act", _act(ft[:, f2, :], ft[:, f, :], AF.Sqrt), reads=[("ft", f)], writes=[("ft", f2)])
            S.op("dve", _recip(qe[:, :], ft[:, f2, :]), reads=[("ft", f2)], writes=[("qe",)])

        def epilogue_chunks(j, engines):
            def mk(c):
                def fn():
                    e = engines[c % len(engines)]
                    S.op(e, _stt(zbuf[:, c, :], zbuf[:, c, :], pcol(cfg.o_g4 + c), qe[:, :], ALU.mult, ALU.mult),
                         reads=[("zb", c), ("qe",), ("P",)], writes=[("zb", c)])
                    S.op(e, _tt(zbuf[:, c, :], zbuf[:, c, :], hbuf[:, c, :], ALU.add),
                         reads=[("zb", c), ("hb", c)], writes=[("zb", c)])
                    if c % 4 == 3:
                        g = c // 4
                        S.dma("sp", _dma(outT_v[:, 4 * g:4 * g + 4, j * T:(j + 1) * T], zbuf[:, 4 * g:4 * g + 4, :]),
                              f"st{g}", reads=[("zb", cc) for cc in range(4 * g, 4 * g + 4)])
                        if j + 1 < NT:
                            ld_h(j + 1, g)
                return fn
            return [mk(c) for c in range(DC)]

        build_ident(S, es, nc, ident)

        for g in range(DC // 4):
            ld_h(0, g)
        for c in range(DC):
            q = sq_next()
            S.op("act", _act(sq[:, q, :], hbuf[:, c, :], AF.Square), reads=[("hb", c)], writes=[("sq", q)])
            stat(SA, onesD, ("onesD",), q, c == 0, c == DC - 1)
        prep_a_fin()
        for c in range(DC):
            S.op("dve", _stt(ab[:, c, :], hbuf[:, c, :], pcol(cfg.o_g1 + c), rX[:, :], ALU.mult, ALU.mult),
                 reads=[("hb", c), ("rX",), ("P",)], writes=[("ab", c)])
        epi = []
        for j in range(NT):
            s1(j, epi)
            s2(j)
            s34(j)
            epilogue_head(j)
            epi = epilogue_chunks(j, ["dve"])
        while epi:
            epi.pop(0)()

        def replay(stream, h, final_waits=()):
            waited = {}
            for waits, fn, inc, dsem in stream.ops:
                mx = {}
                for k, v in waits:
                    if mx.get(k, 0) < v:
                        mx[k] = v
                for k, v in mx.items():
                    if waited.get(k, 0) >= v:
                        continue
                    h.wait_ge(sems[k], v)
                    waited[k] = v
                inst = fn(h)
                if inc:
                    inst.then_inc(sems[stream.name], 1)
                elif dsem is not None:
                    inst.then_inc(sems[dsem], 16)
            for k, v in final_waits:
                h.wait_ge(sems[k], v)

        @block.tensor
        def _(h):
            replay(S.E["pe"], h)

        @block.scalar
        def _(h):
            replay(S.E["act"], h)

        @block.vector
        def _(h):
            replay(S.E["dve"], h)

        @block.gpsimd
        def _(h):
            replay(S.E["pool"], h)

        @block.sync
        def _(h):
            fin = [(f"st{g}", 16 * S.dcount[f"st{g}"]) for g in range(DC // 4)]
            replay(S.E["sp"], h, fin)

    return nc


def build_ident(S, es, nc, ident):
    I32 = mybir.dt.int32
    io_f = es.enter_context(nc.sbuf_tensor("io_f", [128, 128], I32))
    io_p = es.enter_context(nc.sbuf_tensor("io_p", [128, 1], I32))
    io_pf = es.enter_context(nc.sbuf_tensor("io_pf", [128, 1], F32))
    S.op("pool", lambda h: h.iota(io_f[:, :], [[1, 128]], base=0, channel_multiplier=0), writes=[("io_f",)])
    S.op("pool", lambda h: h.iota(io_p[:, :], [[1, 1]], base=0, channel_multiplier=1), writes=[("io_p",)])
    S.op("dve", _copy(io_pf[:, :], io_p[:, :]), reads=[("io_p",)], writes=[("io_pf",)])
    S.op("dve", _ts(ident[:, :], io_f[:, :], io_pf[:, 0:1], None, ALU.is_equal),
         reads=[("io_f",), ("io_pf",)], writes=[("ident",)])


_FULL = Cfg()


def kernel(x, mix_pre_gain, w_in, conv_a_w, conv_b_w, conv_b_bias, ln_b_gain, ln_b_bias, w_out,
           mix_post_gain, mlp_pre_gain, w_up, w_down, mlp_post_gain):
    cfg = _FULL
    f = lambda a: np.asarray(a, dtype=np.float32)
    x = f(x)
    nb = x.shape[0]
    prm = pack_params(cfg, f(mix_pre_gain)[0], f(mix_post_gain)[0], f(mlp_pre_gain)[0], f(mlp_post_gain)[0],
                      f(conv_a_w)[0], f(conv_b_w)[0], f(conv_b_bias)[0], f(ln_b_gain)[0], f(ln_b_bias)[0])
    wall = pack_weights(cfg, f(w_in)[0], f(w_out)[0], f(w_up)[0], f(w_down)[0])
    nc = build_program(cfg)
    in_maps = [{"xT": np.ascontiguousarray(x[b].T), "wall": wall, "prm": prm} for b in range(nb)]
    res = run_bass_kernel_spmd(nc, in_maps, core_ids=list(range(nb)))
    out = np.empty_like(x)
    for b in range(nb):
        out[b] = res.results[b]["outT"].T
    return out
```

```python
from contextlib import ExitStack

import numpy as np
import concourse.bass as bass
import concourse.mybir as mybir
from concourse.bass_utils import run_bass_kernel_spmd

F32 = mybir.dt.float32
BF16 = mybir.dt.bfloat16
ALU = mybir.AluOpType
AF = mybir.ActivationFunctionType


class Cfg:
    def __init__(self, D=2048, DA=1024, DB=1024, DFF=8192, S=2048, T=512, KS=16, KA=3, KB=31,
                 NSLOT=8, NXS=3, NSQ=6, NF=6, EPS=1e-6):
        self.D, self.DA, self.DB, self.DFF, self.S, self.T = D, DA, DB, DFF, S, T
        self.KS, self.KA, self.KB, self.NSLOT, self.NXS, self.NSQ, self.NF, self.EPS = KS, KA, KB, NSLOT, NXS, NSQ, NF, EPS
        self.DC, self.AC, self.BC, self.FC = D // 128, DA // 128, DB // 128, DFF // 128
        self.MC = self.AC + self.BC
        self.DIN = 3 * DA + 2 * DB
        self.NT = S // T
        self.FH = self.FC // 2
        self.HB = KB - 1
        assert self.DC % KS == 0 and self.MC % KS == 0 and self.FH % KS == 0
        assert self.FH >= self.MC + 2 * self.BC and self.DC % 4 == 0
        self.NU = self.DIN // 128 * (self.DC // KS) + self.DC * (self.MC // KS) \
            + 2 * (self.FH * (self.DC // KS) + self.DC * (self.FH // KS))
        DC, AC, BC = self.DC, self.AC, self.BC
        self.o_g1, self.o_g2, self.o_g3, self.o_g4 = 0, DC, 2 * DC, 3 * DC
        self.o_wa = 4 * DC
        self.o_wb = self.o_wa + AC * KA
        self.o_bb = self.o_wb + BC * KB
        self.o_lg = self.o_bb + BC
        self.o_lb = self.o_lg + BC
        self.NP = self.o_lb + BC


def _chunkvec(v):
    return np.ascontiguousarray(v.reshape(-1, 128).T)


def pack_params(cfg, g1, g2, g3, g4, wa, wb, bb, lg, lb):
    cols = [_chunkvec(g1), _chunkvec(g2), _chunkvec(g3), _chunkvec(g4),
            wa.T.reshape(cfg.AC, 128, cfg.KA).transpose(1, 0, 2).reshape(128, -1),
            wb.T.reshape(cfg.BC, 128, cfg.KB).transpose(1, 0, 2).reshape(128, -1),
            _chunkvec(bb), _chunkvec(lg), _chunkvec(lb)]
    p = np.ascontiguousarray(np.concatenate(cols, axis=1).astype(np.float32))
    assert p.shape == (128, cfg.NP)
    return p


def pack_weights(cfg, w_in, w_out, w_up, w_down):
    KS = cfg.KS
    wall = np.empty((cfg.NU, 128, KS * 128), np.float32)
    u = 0

    def put(W, k0, nb):
        nonlocal u
        blk = W[k0 * 128:(k0 + KS) * 128, nb * 128:(nb + 1) * 128]
        wall[u] = blk.reshape(KS, 128, 128).transpose(1, 0, 2).reshape(128, KS * 128)
        u += 1

    DA, DB = cfg.DA, cfg.DB
    for m in s1_colblocks(cfg):
        for s in range(cfg.DC // KS):
            put(w_in, s * KS, m)
    for n in range(cfg.DC):
        for s in range(cfg.MC // KS):
            put(w_out, s * KS, n)
    for half in range(2):
        for fb in range(cfg.FH):
            for s in range(cfg.DC // KS):
                put(w_up, s * KS, half * cfg.FH + fb)
        for n in range(cfg.DC):
            for s in range(cfg.FH // KS):
                put(w_down, half * cfg.FH + s * KS, n)
    assert u == cfg.NU
    return wall


def s1_colblocks(cfg):
    AC, BC = cfg.AC, cfg.BC
    out = []
    for c in range(BC):
        out += [3 * AC + c, 3 * AC + BC + c]
    for c in range(AC):
        out += [AC + c, 2 * AC + c, c]
    return out


class EngStream:
    def __init__(self, name):
        self.name, self.ops, self.count = name, [], 0

    def add(self, fn, waits, inc=True):
        if inc:
            self.count += 1
        self.ops.append((tuple(waits), fn, inc, None))
        return (self.name, self.count) if inc else None

    def add_dma(self, fn, waits, sem):
        self.ops.append((tuple(waits), fn, False, sem))


class Sched:
    def __init__(self):
        self.E = {n: EngStream(n) for n in ("pe", "act", "dve", "pool", "sp")}
        self.track = {}
        self.const = set()
        self.dcount = {}

    def deps(self, reads, writes):
        w = []
        for k in reads:
            st = self.track.get(k)
            if st and st[0] is not None:
                w.append(st[0])
        for k in writes:
            st = self.track.get(k)
            if st:
                if st[0] is not None:
                    w.append(st[0])
                w.extend(st[1].items())
        return w

    def commit(self, tok, reads, writes):
        for k in reads:
            if k in self.const:
                continue
            st = self.track.setdefault(k, [None, {}])
            if st[1].get(tok[0], 0) < tok[1]:
                st[1][tok[0]] = tok[1]
        for k in writes:
            self.track[k] = [tok, {}]

    def op(self, eng, fn, reads=(), writes=()):
        tok = self.E[eng].add(fn, self.deps(reads, writes), True)
        self.commit(tok, reads, writes)
        return tok

    def dma(self, eng, fn, sem, reads=(), writes=()):
        n = self.dcount.get(sem, 0) + 1
        self.dcount[sem] = n
        self.E[eng].add_dma(fn, self.deps(reads, writes), sem)
        tok = (sem, 16 * n)
        self.commit(tok, reads, writes)
        return tok

    def mm_group(self, out_ap, bank_key, items):
        pe = self.E["pe"]
        n = len(items)
        allreads = []
        tok = None
        for i, (lhsT, rhs, reads) in enumerate(items):
            waits = self.deps(reads, [bank_key] if i == 0 else [])
            last = i == n - 1
            tok = pe.add(_mm(out_ap, lhsT, rhs, i == 0, last), waits, inc=last)
            allreads += list(reads)
        self.commit(tok, allreads, [bank_key])
        return tok

    def stat_mm(self, out_ap, bank_key, lhsT, rhs, first, last, reads):
        waits = self.deps(reads, [bank_key] if first else [])
        tok = self.E["pe"].add(_mm(out_ap, lhsT, rhs, first, last), waits, inc=True)
        self.commit(tok, reads, [bank_key])
        return tok


def _mm(out_ap, lhsT, rhs, start, stop):
    return lambda h: h.matmul(out_ap, lhsT, rhs, start=start, stop=stop)


def _act(out, in_, func, bias=None, scale=None):
    kw = {}
    if bias is not None:
        kw["bias"] = bias
    if scale is not None:
        kw["scale"] = scale
    return lambda h: h.activation(out=out, in_=in_, func=func, **kw)


def _tt(out, in0, in1, op):
    return lambda h: h.tensor_tensor(out=out, in0=in0, in1=in1, op=op)


def _stt(out, in0, scalar, in1, op0, op1):
    return lambda h: h.scalar_tensor_tensor(out=out, in0=in0, scalar=scalar, in1=in1, op0=op0, op1=op1)


def _ts(out, in0, s1, s2, op0, op1=None):
    if op1 is None:
        return lambda h: h.tensor_scalar(out=out, in0=in0, scalar1=s1, scalar2=None, op0=op0)
    return lambda h: h.tensor_scalar(out=out, in0=in0, scalar1=s1, scalar2=s2, op0=op0, op1=op1)


def _copy(out, in_):
    return lambda h: h.tensor_copy(out, in_)


def _recip(out, in_):
    return lambda h: h.reciprocal(out=out, in_=in_)


def _memset(ap, v):
    return lambda h: h.memset(ap, v)


def _dma(out, in_):
    return lambda h: h.dma_start(out=out, in_=in_)


def build_program(cfg):
    D, S_, T = cfg.D, cfg.S, cfg.T
    DC, AC, BC, MC, FH, KS, KA, KB, HB = cfg.DC, cfg.AC, cfg.BC, cfg.MC, cfg.FH, cfg.KS, cfg.KA, cfg.KB, cfg.HB
    NSLOT, NXS, NSQ, NF, EPS, NT, NP, NU = cfg.NSLOT, cfg.NXS, cfg.NSQ, cfg.NF, cfg.EPS, cfg.NT, cfg.NP, cfg.NU

    nc = bass.Bass("TRN2", target_bir_lowering=False)
    xT = nc.dram_tensor("xT", [D, S_], F32, kind="ExternalInput")
    wall = nc.dram_tensor("wall", [NU, 128, KS * 128], F32, kind="ExternalInput")
    prm = nc.dram_tensor("prm", [128, NP], F32, kind="ExternalInput")
    outT = nc.dram_tensor("outT", [D, S_], F32, kind="ExternalOutput")
    outT_v = outT.rearrange("(c p) t -> p c t", p=128)

    with ExitStack() as es:
        def sb(name, shape, dt):
            return es.enter_context(nc.sbuf_tensor(name, shape, dt))

        P = sb("P", [128, NP], F32)
        onesD = sb("onesD", [128, 128], BF16)
        onesB = sb("onesB", [128, 128], BF16)
        ident = sb("ident", [128, 128], BF16)
        epsT = sb("epsT", [128, 1], F32)
        dummy = sb("scr2", [128, 2], F32)
        hbuf = sb("hbuf", [128, DC, T], F32)
        zbuf = sb("zbuf", [128, DC, T], F32)
        ab = sb("ab", [128, DC, T], BF16)
        hid = sb("hid", [128, FH, T], BF16)
        gb = sb("gb", [128, BC, HB + T], BF16)
        dg = sb("dg", [128, 2, KB, 128], BF16)
        xs = sb("xs", [128, NXS, T], F32)
        sq = sb("sq", [128, NSQ, T], BF16)
        wsl = sb("wsl", [128, NSLOT, KS * 128], BF16)
        cvw = sb("cvw", [128, 2, T + KA - 1], F32)
        cvh = sb("cvh", [128, AC, KA - 1], F32)
        ft = sb("ft", [128, NF, T], F32)
        rX = sb("rX", [128, T], F32)
        nmr = sb("nmr", [128, T], F32)
        qe = sb("qe", [128, T], F32)
        ps = [es.enter_context(nc.psum_tensor(f"ps{b}", [128, T], F32)) for b in range(8)]
        hidF = hid.bitcast(F32)

        def G(c):
            return bass.AP(hidF, MC * (T // 2) + c * T, [[FH * (T // 2), 128], [1, T]])

        def Gkeys(c):
            return [("hid", MC + 2 * c), ("hid", MC + 2 * c + 1)]

        def pcol(off):
            return P[:, off:off + 1]

        semnames = ["pe", "act", "dve", "pool", "sp", "prm"] + [f"w{i}" for i in range(NSLOT)] \
            + [f"xs{i}" for i in range(NXS)] + [f"st{i}" for i in range(DC // 4)] + [f"hx{i}" for i in range(DC // 4)]
        sems = {n: es.enter_context(nc.semaphore(n)) for n in semnames}
        block = es.enter_context(nc.Block())

        S = Sched()
        S.const |= {("P",), ("onesD",), ("onesB",), ("ident",), ("eps",)}
        SA, SB = 6, 7
        state = {"bank": 0, "xs": 0, "sq": 0, "ft": 0, "wl": 0, "wu": 0}
        sq_pending = [False] * NSQ

        def bank():
            b = state["bank"]
            state["bank"] = (b + 1) % 6
            return b

        def xs_next():
            k = state["xs"]
            state["xs"] = (k + 1) % NXS
            return k

        def sq_next():
            q = state["sq"]
            state["sq"] = (q + 1) % NSQ
            assert not sq_pending[q], "sq slot reused before its consumer was emitted"
            sq_pending[q] = True
            return q

        def ft_next():
            f = state["ft"]
            state["ft"] = (f + 1) % NF
            return f

        def stat(bk, ones, ones_key, q, first, last):
            S.stat_mm(ps[bk][:, :], ("ps", bk), ones[:, :], sq[:, q, :], first, last, [("sq", q), ones_key])
            sq_pending[q] = False

        total_loads = NU * NT

        def w_issue():
            l = state["wl"]
            if l >= total_loads:
                return
            slot, u = l % NSLOT, l % NU
            S.dma("pool", _dma(wsl[:, slot, :], wall[u, :, :]), f"w{slot}", writes=[("w", slot)])
            state["wl"] = l + 1

        def w_take(n):
            l = state["wu"]
            state["wu"] = l + n
            assert state["wu"] <= state["wl"]
            return [(l + i) % NSLOT for i in range(n)]

        def wgroup(bk, nk, rhs_of, rkey_of):
            slots = w_take(nk // KS)
            items = []
            for kk in range(nk):
                s = slots[kk // KS]
                o = (kk % KS) * 128
                items.append((wsl[:, s, o:o + 128], rhs_of(kk), [("w", s), rkey_of(kk)]))
            tok = S.mm_group(ps[bk][:, :], ("ps", bk), items)
            for _ in slots:
                w_issue()
            return tok

        def ld_x(j, c, k):
            S.dma("sp", _dma(xs[:, k, :], xT[c * 128:(c + 1) * 128, j * T:(j + 1) * T]), f"xs{k}",
                  writes=[("xs", k)])

        xT_v = xT.rearrange("(c p) t -> p c t", p=128)
        S.dma("sp", _dma(P[:, :], prm[:, :]), "prm", writes=[("P",)])
        S.op("dve", _memset(onesD[:, :], 1.0 / D), writes=[("onesD",)])
        S.op("dve", _memset(onesB[:, :], 1.0 / cfg.DB), writes=[("onesB",)])
        S.op("dve", _memset(epsT[:, :], EPS), writes=[("eps",)])
        for c in range(BC):
            S.op("dve", _memset(gb[:, c, 0:HB], 0.0), writes=[("gbh", c)])
        for c in range(AC):
            S.op("dve", _memset(cvh[:, c, :], 0.0), writes=[("cvh", c)])

        for _ in range(NSLOT):
            w_issue()

        def ld_h(j, g):
            S.dma("sp", _dma(hbuf[:, 4 * g:4 * g + 4, :], xT_v[:, 4 * g:4 * g + 4, j * T:(j + 1) * T]), f"hx{g}",
                  writes=[("hb", c) for c in range(4 * g, 4 * g + 4)])

        def prep_a_chunk(j, c):
            k = xs_next()
            ld_x(j, c, k)
            q = sq_next()
            S.op("act", _act(sq[:, q, :], xs[:, k, :], AF.Square), reads=[("xs", k)], writes=[("sq", q)])
            return q

        def prep_a_fin():
            f = ft_next()
            S.op("act", _act(ft[:, f, :], ps[SA][:, :], AF.Sqrt, bias=epsT[:, 0:1]),
                 reads=[("ps", SA), ("eps",)], writes=[("ft", f)])
            S.op("dve", _recip(rX[:, :], ft[:, f, :]), reads=[("ft", f)], writes=[("rX",)])

        def prep_b_chunk(j, c):
            k = xs_next()
            ld_x(j, c, k)
            S.op("dve", _stt(ab[:, c, :], xs[:, k, :], pcol(cfg.o_g1 + c), rX[:, :], ALU.mult, ALU.mult),
                 reads=[("xs", k), ("rX",), ("P",)], writes=[("ab", c)])

        def cdiv(a, b):
            return -(-a // b)

        def s1(j, epi):
            pend = []

            def tick():
                for p in pend:
                    p[0] -= 1
                while pend and pend[0][0] <= 0:
                    pend.pop(0)[1]()

            def flush():
                while pend:
                    tick()

            def conv(c):
                bk = bank()
                b = c % 2
                items = [(dg[:, b, k, :], gb[:, c, k:k + T], [("dg", b), ("gbc", c), ("gbh", c)]) for k in range(KB)]
                S.mm_group(ps[bk][:, :], ("ps", bk), items)
                S.op("dve", _copy(gb[:, c, 0:HB], gb[:, c, T:T + HB]), reads=[("gbc", c)], writes=[("gbh", c)])
                S.op("act", _act(G(c), ps[bk][:, :], AF.Identity, bias=pcol(cfg.o_bb + c)),
                     reads=[("ps", bk), ("P",)], writes=Gkeys(c))
                q1 = sq_next()
                S.op("act", _act(sq[:, q1, :], G(c), AF.Identity), reads=Gkeys(c), writes=[("sq", q1)])
                q2 = sq_next()
                S.op("act", _act(sq[:, q2, :], G(c), AF.Square), reads=Gkeys(c), writes=[("sq", q2)])

                def st():
                    stat(SA, onesB, ("onesB",), q1, c == 0, c == BC - 1)
                    stat(SB, onesB, ("onesB",), q2, c == 0, c == BC - 1)
                pend.append([1, st])

            def lnfin():
                fm = ft_next()
                S.op("act", _act(ft[:, fm, :], ps[SA][:, :], AF.Identity), reads=[("ps", SA)], writes=[("ft", fm)])
                f2 = ft_next()
                S.op("dve", _tt(ft[:, f2, :], ft[:, fm, :], ft[:, fm, :], ALU.mult), reads=[("ft", fm)], writes=[("ft", f2)])
                f3 = ft_next()
                S.op("dve", _tt(ft[:, f3, :], ps[SB][:, :], ft[:, f2, :], ALU.subtract),
                     reads=[("ps", SB), ("ft", f2)], writes=[("ft", f3)])
                f4 = ft_next()
                S.op("act", _act(ft[:, f4, :], ft[:, f3, :], AF.Sqrt, bias=epsT[:, 0:1]),
                     reads=[("ft", f3), ("eps",)], writes=[("ft", f4)])
                S.op("dve", _recip(rX[:, :], ft[:, f4, :]), reads=[("ft", f4)], writes=[("rX",)])
                S.op("dve", _stt(nmr[:, :], ft[:, fm, :], -1.0, rX[:, :], ALU.mult, ALU.mult),
                     reads=[("ft", fm), ("rX",)], writes=[("nmr",)])

            def lnapply(c):
                f = ft_next()
                S.op("dve", _tt(ft[:, f, :], G(c), rX[:, :], ALU.mult), reads=Gkeys(c) + [("rX",)], writes=[("ft", f)])
                S.op("dve", _tt(ft[:, f, :], ft[:, f, :], nmr[:, :], ALU.add), reads=[("ft", f), ("nmr",)], writes=[("ft", f)])
                S.op("act", _act(hid[:, AC + c, :], ft[:, f, :], AF.Silu, bias=pcol(cfg.o_lb + c), scale=pcol(cfg.o_lg + c)),
                     reads=[("ft", f), ("P",)], writes=[("hid", AC + c)])

            for c in range(BC):
                bv, bg = bank(), bank()
                wgroup(bv, DC, lambda kk: ab[:, kk, :], lambda kk: ("ab", kk))
                wgroup(bg, DC, lambda kk: ab[:, kk, :], lambda kk: ("ab", kk))
                f = ft_next()
                S.op("act", _act(ft[:, f, :], ps[bg][:, :], AF.Sigmoid), reads=[("ps", bg)], writes=[("ft", f)])
                S.op("dve", _tt(gb[:, c, HB:HB + T], ps[bv][:, :], ft[:, f, :], ALU.mult),
                     reads=[("ps", bv), ("ft", f)], writes=[("gbc", c)])
                b = c % 2
                wbc = P[:, cfg.o_wb + c * KB: cfg.o_wb + (c + 1) * KB].unsqueeze(2).broadcast_to([128, KB, 128])
                idc = ident[:, :].unsqueeze(1).broadcast_to([128, KB, 128])
                S.op("dve", _tt(dg[:, b, :, :], idc, wbc, ALU.mult), reads=[("ident",), ("P",)], writes=[("dg", b)])
                pend.append([2, (lambda c=c: conv(c))])
                tick()
                for _ in range(cdiv(len(epi), BC - c)):
                    epi.pop(0)()

            ln = {"fin": False, "next": 0}
            for i in range(AC):
                bC, bV, bB = bank(), bank(), bank()
                wgroup(bC, DC, lambda kk: ab[:, kk, :], lambda kk: ("ab", kk))
                wgroup(bV, DC, lambda kk: ab[:, kk, :], lambda kk: ("ab", kk))
                wgroup(bB, DC, lambda kk: ab[:, kk, :], lambda kk: ("ab", kk))
                c = i
                f1 = ft_next()
                S.op("act", _act(ft[:, f1, :], ps[bC][:, :], AF.Identity), reads=[("ps", bC)], writes=[("ft", f1)])
                w = c % 2
                S.op("dve", _copy(cvw[:, w, 0:KA - 1], cvh[:, c, :]), reads=[("cvh", c)], writes=[("cvw", w)])
                S.op("dve", _tt(cvw[:, w, KA - 1:KA - 1 + T], ps[bV][:, :], ft[:, f1, :], ALU.mult),
                     reads=[("ps", bV), ("ft", f1)], writes=[("cvw", w)])
                f2 = ft_next()
                S.op("dve", _ts(ft[:, f2, :], cvw[:, w, KA - 1:KA - 1 + T], pcol(cfg.o_wa + c * KA + KA - 1), None, ALU.mult),
                     reads=[("cvw", w), ("P",)], writes=[("ft", f2)])
                for k in range(KA - 2, -1, -1):
                    S.op("dve", _stt(ft[:, f2, :], cvw[:, w, k:k + T], pcol(cfg.o_wa + c * KA + k), ft[:, f2, :], ALU.mult, ALU.add),
                         reads=[("cvw", w), ("ft", f2), ("P",)], writes=[("ft", f2)])
                S.op("dve", _tt(hid[:, c, :], ps[bB][:, :], ft[:, f2, :], ALU.mult),
                     reads=[("ps", bB), ("ft", f2)], writes=[("hid", c)])
                S.op("dve", _copy(cvh[:, c, :], cvw[:, w, T:T + KA - 1]), reads=[("cvw", w)], writes=[("cvh", c)])
                tick()
                for _ in range(cdiv(len(epi), AC - i)):
                    epi.pop(0)()
                if not ln["fin"]:
                    if not pend:
                        lnfin()
                        ln["fin"] = True
                else:
                    for _ in range(cdiv(BC - ln["next"], max(1, AC - 2 - i))):
                        if ln["next"] < BC:
                            lnapply(ln["next"])
                            ln["next"] += 1
            flush()
            while epi:
                epi.pop(0)()
            if not ln["fin"]:
                lnfin()
            while ln["next"] < BC:
                lnapply(ln["next"])
                ln["next"] += 1
            S.op("act", _act(dummy[:, 1:2], epsT[:, 0:1], AF.Sqrt), reads=[("eps",)], writes=[("dummy",)])

        def s2(j):
            pend = None
            for n in range(DC):
                bk = bank()
                wgroup(bk, MC, lambda kk: hid[:, kk, :], lambda kk: ("hid", kk))
                if pend is not None:
                    pend()
                S.op("act", _act(zbuf[:, n, :], ps[bk][:, :], AF.Identity, scale=pcol(cfg.o_g2 + n)),
                     reads=[("ps", bk), ("P",)], writes=[("zb", n)])
                q = sq_next()
                S.op("act", _act(sq[:, q, :], ps[bk][:, :], AF.Square), reads=[("ps", bk)], writes=[("sq", q)])
                pend = (lambda q=q, n=n: stat(SA, onesD, ("onesD",), q, n == 0, n == DC - 1))
            pend()
            f = ft_next()
            S.op("act", _act(ft[:, f, :], ps[SA][:, :], AF.Sqrt, bias=epsT[:, 0:1]),
                 reads=[("ps", SA), ("eps",)], writes=[("ft", f)])
            S.op("dve", _recip(rX[:, :], ft[:, f, :]), reads=[("ft", f)], writes=[("rX",)])
            for c in range(DC):
                e = "dve"
                S.op(e, _tt(zbuf[:, c, :], zbuf[:, c, :], rX[:, :], ALU.mult),
                     reads=[("zb", c), ("rX",)], writes=[("zb", c)])
                S.op(e, _tt(hbuf[:, c, :], zbuf[:, c, :], hbuf[:, c, :], ALU.add),
                     reads=[("zb", c), ("hb", c)], writes=[("hb", c)])
                S.op("act", _act(ab[:, c, :], hbuf[:, c, :], AF.Identity, scale=pcol(cfg.o_g3 + c)),
                     reads=[("hb", c), ("P",)], writes=[("ab", c)])

        NG3 = min(5, FH, DC)

        def s3_first(ng):
            banks = [bank() for _ in range(ng)]
            slots = [w_take(DC // KS) for _ in range(ng)]
            pe = S.E["pe"]
            toks = [None] * ng
            reads = [[] for _ in range(ng)]
            for kk in range(DC):
                for g in range(ng):
                    sl = slots[g][kk // KS]
                    o = (kk % KS) * 128
                    rd = [("w", sl), ("ab", kk)]
                    bk = banks[g]
                    waits = S.deps(rd, [("ps", bk)] if kk == 0 else [])
                    last = kk == DC - 1
                    t = pe.add(_mm(ps[bk][:, :], wsl[:, sl, o:o + 128], ab[:, kk, :], kk == 0, last), waits, inc=last)
                    reads[g] += rd
                    if last:
                        toks[g] = t
            for g in range(ng):
                S.commit(toks[g], reads[g], [("ps", banks[g])])
            for g in range(ng):
                for _ in slots[g]:
                    w_issue()
            for g in range(ng):
                fb = g
                qh = sq_next()
                S.op("act", _act(sq[:, qh, :], hbuf[:, fb, :], AF.Square), reads=[("hb", fb)], writes=[("sq", qh)])
                f = ft_next()
                S.op("act", _act(ft[:, f, :], ps[banks[g]][:, :], AF.Relu), reads=[("ps", banks[g])], writes=[("ft", f)])
                S.op("dve", _tt(hid[:, fb, :], ft[:, f, :], ft[:, f, :], ALU.mult), reads=[("ft", f)], writes=[("hid", fb)])
                stat(SB, onesD, ("onesD",), qh, fb == 0, fb == DC - 1)
                if fb == DC - 1:
                    S.op("act", _act(qe[:, :], ps[SB][:, :], AF.Square, bias=epsT[:, 0:1]),
                         reads=[("ps", SB), ("eps",)], writes=[("qe",)])

        def s34(j):
            nxt = j + 1 < NT
            for half in range(2):
                fb0 = 0
                if half == 0:
                    fb0 = NG3
                    s3_first(NG3)
                for fb in range(fb0, FH):
                    qh = None
                    if half == 0 and fb < DC:
                        qh = sq_next()
                        S.op("act", _act(sq[:, qh, :], hbuf[:, fb, :], AF.Square), reads=[("hb", fb)], writes=[("sq", qh)])
                    bk = bank()
                    wgroup(bk, DC, lambda kk: ab[:, kk, :], lambda kk: ("ab", kk))
                    if qh is not None:
                        stat(SB, onesD, ("onesD",), qh, fb == 0, fb == DC - 1)
                        if fb == DC - 1:
                            S.op("act", _act(qe[:, :], ps[SB][:, :], AF.Square, bias=epsT[:, 0:1]),
                                 reads=[("ps", SB), ("eps",)], writes=[("qe",)])
                    f = ft_next()
                    S.op("act", _act(ft[:, f, :], ps[bk][:, :], AF.Relu), reads=[("ps", bk)], writes=[("ft", f)])
                    S.op("dve", _tt(hid[:, fb, :], ft[:, f, :], ft[:, f, :], ALU.mult), reads=[("ft", f)], writes=[("hid", fb)])
                pend = None
                for n in range(DC):
                    qx = None
                    if nxt and half == 0:
                        qx = prep_a_chunk(j + 1, n)
                    if nxt and half == 1:
                        prep_b_chunk(j + 1, n)
                    bk = bank()
                    wgroup(bk, FH, lambda kk: hid[:, kk, :], lambda kk: ("hid", kk))
                    if qx is not None:
                        stat(SA, onesD, ("onesD",), qx, n == 0, n == DC - 1)
                    if pend is not None:
                        pend()
                        pend = None
                    if half == 0:
                        S.op("act", _act(zbuf[:, n, :], ps[bk][:, :], AF.Identity), reads=[("ps", bk)], writes=[("zb", n)])
                    else:
                        S.op("dve", _tt(zbuf[:, n, :], ps[bk][:, :], zbuf[:, n, :], ALU.add),
                             reads=[("ps", bk), ("zb", n)], writes=[("zb", n)])
                        q = sq_next()
                        S.op("act", _act(sq[:, q, :], zbuf[:, n, :], AF.Square), reads=[("zb", n)], writes=[("sq", q)])
                        pend = (lambda q=q, n=n: stat(SB, onesD, ("onesD",), q, n == 0, n == DC - 1))
                if pend is not None:
                    pend()
                if half == 0 and nxt:
                    prep_a_fin()

        def epilogue_head(j):
            f = ft_next()
            S.op("dve", _stt(ft[:, f, :], qe[:, :], EPS, ps[SB][:, :], ALU.mult, ALU.add),
                 reads=[("qe",), ("ps", SB)], writes=[("ft", f)])
            f2 = ft_next()
            S.op("act", _act(ft[:, f2, :], ft[:, f, :], AF.Sqrt), reads=[("ft", f)], writes=[("ft", f2)])
            S.op("dve", _recip(qe[:, :], ft[:, f2, :]), reads=[("ft", f2)], writes=[("qe",)])

        def epilogue_chunks(j, engines):
            def mk(c):
                def fn():
                    e = engines[c % len(engines)]
                    S.op(e, _stt(zbuf[:, c, :], zbuf[:, c, :], pcol(cfg.o_g4 + c), qe[:, :], ALU.mult, ALU.mult),
                         reads=[("zb", c), ("qe",), ("P",)], writes=[("zb", c)])
                    S.op(e, _tt(zbuf[:, c, :], zbuf[:, c, :], hbuf[:, c, :], ALU.add),
                         reads=[("zb", c), ("hb", c)], writes=[("zb", c)])
                    if c % 4 == 3:
                        g = c // 4
                        S.dma("sp", _dma(outT_v[:, 4 * g:4 * g + 4, j * T:(j + 1) * T], zbuf[:, 4 * g:4 * g + 4, :]),
                              f"st{g}", reads=[("zb", cc) for cc in range(4 * g, 4 * g + 4)])
                        if j + 1 < NT:
                            ld_h(j + 1, g)
                return fn
            return [mk(c) for c in range(DC)]

        build_ident(S, es, nc, ident)

        for g in range(DC // 4):
            ld_h(0, g)
        for c in range(DC):
            q = sq_next()
            S.op("act", _act(sq[:, q, :], hbuf[:, c, :], AF.Square), reads=[("hb", c)], writes=[("sq", q)])
            stat(SA, onesD, ("onesD",), q, c == 0, c == DC - 1)
        prep_a_fin()
        for c in range(DC):
            S.op("dve", _stt(ab[:, c, :], hbuf[:, c, :], pcol(cfg.o_g1 + c), rX[:, :], ALU.mult, ALU.mult),
                 reads=[("hb", c), ("rX",), ("P",)], writes=[("ab", c)])
        epi = []
        for j in range(NT):
            s1(j, epi)
            s2(j)
            s34(j)
            epilogue_head(j)
            epi = epilogue_chunks(j, ["dve"])
        while epi:
            epi.pop(0)()

        def replay(stream, h, final_waits=()):
            waited = {}
            for waits, fn, inc, dsem in stream.ops:
                mx = {}
                for k, v in waits:
                    if mx.get(k, 0) < v:
                        mx[k] = v
                for k, v in mx.items():
                    if waited.get(k, 0) >= v:
                        continue
                    h.wait_ge(sems[k], v)
                    waited[k] = v
                inst = fn(h)
                if inc:
                    inst.then_inc(sems[stream.name], 1)
                elif dsem is not None:
                    inst.then_inc(sems[dsem], 16)
            for k, v in final_waits:
                h.wait_ge(sems[k], v)

        @block.tensor
        def _(h):
            replay(S.E["pe"], h)

        @block.scalar
        def _(h):
            replay(S.E["act"], h)

        @block.vector
        def _(h):
            replay(S.E["dve"], h)

        @block.gpsimd
        def _(h):
            replay(S.E["pool"], h)

        @block.sync
        def _(h):
            fin = [(f"st{g}", 16 * S.dcount[f"st{g}"]) for g in range(DC // 4)]
            replay(S.E["sp"], h, fin)

    return nc


def build_ident(S, es, nc, ident):
    I32 = mybir.dt.int32
    io_f = es.enter_context(nc.sbuf_tensor("io_f", [128, 128], I32))
    io_p = es.enter_context(nc.sbuf_tensor("io_p", [128, 1], I32))
    io_pf = es.enter_context(nc.sbuf_tensor("io_pf", [128, 1], F32))
    S.op("pool", lambda h: h.iota(io_f[:, :], [[1, 128]], base=0, channel_multiplier=0), writes=[("io_f",)])
    S.op("pool", lambda h: h.iota(io_p[:, :], [[1, 1]], base=0, channel_multiplier=1), writes=[("io_p",)])
    S.op("dve", _copy(io_pf[:, :], io_p[:, :]), reads=[("io_p",)], writes=[("io_pf",)])
    S.op("dve", _ts(ident[:, :], io_f[:, :], io_pf[:, 0:1], None, ALU.is_equal),
         reads=[("io_f",), ("io_pf",)], writes=[("ident",)])


_FULL = Cfg()


def kernel(x, mix_pre_gain, w_in, conv_a_w, conv_b_w, conv_b_bias, ln_b_gain, ln_b_bias, w_out,
           mix_post_gain, mlp_pre_gain, w_up, w_down, mlp_post_gain):
    cfg = _FULL
    f = lambda a: np.asarray(a, dtype=np.float32)
    x = f(x)
    nb = x.shape[0]
    prm = pack_params(cfg, f(mix_pre_gain)[0], f(mix_post_gain)[0], f(mlp_pre_gain)[0], f(mlp_post_gain)[0],
                      f(conv_a_w)[0], f(conv_b_w)[0], f(conv_b_bias)[0], f(ln_b_gain)[0], f(ln_b_bias)[0])
    wall = pack_weights(cfg, f(w_in)[0], f(w_out)[0], f(w_up)[0], f(w_down)[0])
    nc = build_program(cfg)
    in_maps = [{"xT": np.ascontiguousarray(x[b].T), "wall": wall, "prm": prm} for b in range(nb)]
    res = run_bass_kernel_spmd(nc, in_maps, core_ids=list(range(nb)))
    out = np.empty_like(x)
    for b in range(nb):
        out[b] = res.results[b]["outT"].T
    return out
```

```python
from contextlib import ExitStack

import numpy as np
import concourse.bass as bass
import concourse.mybir as mybir
from concourse.bass_utils import run_bass_kernel_spmd

F32 = mybir.dt.float32
BF16 = mybir.dt.bfloat16
ALU = mybir.AluOpType
AF = mybir.ActivationFunctionType


class Cfg:
    def __init__(self, D=2048, DA=1024, DB=1024, DFF=8192, S=2048, T=512, KS=16, KA=3, KB=31,
                 NSLOT=8, NXS=3, NSQ=6, NF=6, EPS=1e-6, CONV_TILED=True):
        self.CONV_TILED = CONV_TILED
        self.D, self.DA, self.DB, self.DFF, self.S, self.T = D, DA, DB, DFF, S, T
        self.KS, self.KA, self.KB, self.NSLOT, self.NXS, self.NSQ, self.NF, self.EPS = KS, KA, KB, NSLOT, NXS, NSQ, NF, EPS
        self.DC, self.AC, self.BC, self.FC = D // 128, DA // 128, DB // 128, DFF // 128
        self.MC = self.AC + self.BC
        self.DIN = 3 * DA + 2 * DB
        self.NT = S // T
        self.FH = self.FC // 2
        self.HB = KB - 1
        assert self.DC % KS == 0 and self.MC % KS == 0 and self.FH % KS == 0
        assert self.FH >= self.MC + 2 * self.BC and self.DC % 4 == 0
        self.NU = self.DIN // 128 * (self.DC // KS) + self.DC * (self.MC // KS) \
            + 2 * (self.FH * (self.DC // KS) + self.DC * (self.FH // KS))
        DC, AC, BC = self.DC, self.AC, self.BC
        self.o_g1, self.o_g2, self.o_g3, self.o_g4 = 0, DC, 2 * DC, 3 * DC
        self.o_wa = 4 * DC
        self.o_wb = self.o_wa + AC * KA
        self.o_bb = self.o_wb + BC * KB
        self.o_lg = self.o_bb + BC
        self.o_lb = self.o_lg + BC
        self.NP = self.o_lb + BC


def _chunkvec(v):
    return np.ascontiguousarray(v.reshape(-1, 128).T)


def pack_params(cfg, g1, g2, g3, g4, wa, wb, bb, lg, lb):
    cols = [_chunkvec(g1), _chunkvec(g2), _chunkvec(g3), _chunkvec(g4),
            wa.T.reshape(cfg.AC, 128, cfg.KA).transpose(1, 0, 2).reshape(128, -1),
            wb.T.reshape(cfg.BC, 128, cfg.KB).transpose(1, 0, 2).reshape(128, -1),
            _chunkvec(bb), _chunkvec(lg), _chunkvec(lb)]
    p = np.ascontiguousarray(np.concatenate(cols, axis=1).astype(np.float32))
    assert p.shape == (128, cfg.NP)
    return p


def pack_weights(cfg, w_in, w_out, w_up, w_down):
    KS = cfg.KS
    wall = np.empty((cfg.NU, 128, KS * 128), np.float32)
    u = 0

    def put(W, k0, nb):
        nonlocal u
        blk = W[k0 * 128:(k0 + KS) * 128, nb * 128:(nb + 1) * 128]
        wall[u] = blk.reshape(KS, 128, 128).transpose(1, 0, 2).reshape(128, KS * 128)
        u += 1

    DA, DB = cfg.DA, cfg.DB
    for m in s1_colblocks(cfg):
        for s in range(cfg.DC // KS):
            put(w_in, s * KS, m)
    for n in range(cfg.DC):
        for s in range(cfg.MC // KS):
            put(w_out, s * KS, n)
    for half in range(2):
        for fb in range(cfg.FH):
            for s in range(cfg.DC // KS):
                put(w_up, s * KS, half * cfg.FH + fb)
        for n in range(cfg.DC):
            for s in range(cfg.FH // KS):
                put(w_down, half * cfg.FH + s * KS, n)
    assert u == cfg.NU
    return wall


def s1_colblocks(cfg):
    AC, BC = cfg.AC, cfg.BC
    out = []
    for c in range(BC):
        out += [3 * AC + c, 3 * AC + BC + c]
    for c in range(AC):
        out += [AC + c, 2 * AC + c, c]
    return out


class EngStream:
    def __init__(self, name):
        self.name, self.ops, self.count = name, [], 0

    def add(self, fn, waits, inc=True):
        if inc:
            self.count += 1
        self.ops.append((tuple(waits), fn, inc, None))
        return (self.name, self.count) if inc else None

    def add_dma(self, fn, waits, sem):
        self.ops.append((tuple(waits), fn, False, sem))


class Sched:
    def __init__(self):
        self.E = {n: EngStream(n) for n in ("pe", "act", "dve", "pool", "sp")}
        self.track = {}
        self.const = set()
        self.dcount = {}

    def deps(self, reads, writes):
        w = []
        for k in reads:
            st = self.track.get(k)
            if st and st[0] is not None:
                w.append(st[0])
        for k in writes:
            st = self.track.get(k)
            if st:
                if st[0] is not None:
                    w.append(st[0])
                w.extend(st[1].items())
        return w

    def commit(self, tok, reads, writes):
        for k in reads:
            if k in self.const:
                continue
            st = self.track.setdefault(k, [None, {}])
            if st[1].get(tok[0], 0) < tok[1]:
                st[1][tok[0]] = tok[1]
        for k in writes:
            self.track[k] = [tok, {}]

    def op(self, eng, fn, reads=(), writes=()):
        tok = self.E[eng].add(fn, self.deps(reads, writes), True)
        self.commit(tok, reads, writes)
        return tok

    def dma(self, eng, fn, sem, reads=(), writes=()):
        n = self.dcount.get(sem, 0) + 1
        self.dcount[sem] = n
        self.E[eng].add_dma(fn, self.deps(reads, writes), sem)
        tok = (sem, 16 * n)
        self.commit(tok, reads, writes)
        return tok

    def mm_group(self, out_ap, bank_key, items):
        pe = self.E["pe"]
        n = len(items)
        allreads = []
        tok = None
        for i, (lhsT, rhs, reads) in enumerate(items):
            waits = self.deps(reads, [bank_key] if i == 0 else [])
            last = i == n - 1
            tok = pe.add(_mm(out_ap, lhsT, rhs, i == 0, last), waits, inc=last)
            allreads += list(reads)
        self.commit(tok, allreads, [bank_key])
        return tok

    def stat_mm(self, out_ap, bank_key, lhsT, rhs, first, last, reads):
        waits = self.deps(reads, [bank_key] if first else [])
        tok = self.E["pe"].add(_mm(out_ap, lhsT, rhs, first, last), waits, inc=True)
        self.commit(tok, reads, [bank_key])
        return tok


def _mm(out_ap, lhsT, rhs, start, stop):
    return lambda h: h.matmul(out_ap, lhsT, rhs, start=start, stop=stop)


def _mmt(out_ap, lhsT, rhs, start, stop, tp):
    return lambda h: h.matmul(out_ap, lhsT, rhs, start=start, stop=stop, tile_position=tp)


def _act(out, in_, func, bias=None, scale=None):
    kw = {}
    if bias is not None:
        kw["bias"] = bias
    if scale is not None:
        kw["scale"] = scale
    return lambda h: h.activation(out=out, in_=in_, func=func, **kw)


def _tt(out, in0, in1, op):
    return lambda h: h.tensor_tensor(out=out, in0=in0, in1=in1, op=op)


def _stt(out, in0, scalar, in1, op0, op1):
    return lambda h: h.scalar_tensor_tensor(out=out, in0=in0, scalar=scalar, in1=in1, op0=op0, op1=op1)


def _ts(out, in0, s1, s2, op0, op1=None):
    if op1 is None:
        return lambda h: h.tensor_scalar(out=out, in0=in0, scalar1=s1, scalar2=None, op0=op0)
    return lambda h: h.tensor_scalar(out=out, in0=in0, scalar1=s1, scalar2=s2, op0=op0, op1=op1)


def _copy(out, in_):
    return lambda h: h.tensor_copy(out, in_)


def _recip(out, in_):
    return lambda h: h.reciprocal(out=out, in_=in_)


def _memset(ap, v):
    return lambda h: h.memset(ap, v)


def _dma(out, in_):
    return lambda h: h.dma_start(out=out, in_=in_)


def build_program(cfg):
    D, S_, T = cfg.D, cfg.S, cfg.T
    DC, AC, BC, MC, FH, KS, KA, KB, HB = cfg.DC, cfg.AC, cfg.BC, cfg.MC, cfg.FH, cfg.KS, cfg.KA, cfg.KB, cfg.HB
    NSLOT, NXS, NSQ, NF, EPS, NT, NP, NU = cfg.NSLOT, cfg.NXS, cfg.NSQ, cfg.NF, cfg.EPS, cfg.NT, cfg.NP, cfg.NU

    nc = bass.Bass("TRN2", target_bir_lowering=False)
    xT = nc.dram_tensor("xT", [D, S_], F32, kind="ExternalInput")
    wall = nc.dram_tensor("wall", [NU, 128, KS * 128], F32, kind="ExternalInput")
    prm = nc.dram_tensor("prm", [128, NP], F32, kind="ExternalInput")
    outT = nc.dram_tensor("outT", [D, S_], F32, kind="ExternalOutput")
    outT_v = outT.rearrange("(c p) t -> p c t", p=128)

    with ExitStack() as es:
        def sb(name, shape, dt):
            return es.enter_context(nc.sbuf_tensor(name, shape, dt))

        P = sb("P", [128, NP], F32)
        onesD = sb("onesD", [128, 128], BF16)
        onesB = sb("onesB", [128, 128], BF16)
        ident = sb("ident", [128, 128], BF16)
        epsT = sb("epsT", [128, 1], F32)
        dummy = sb("scr2", [128, 2], F32)
        smat = sb("smat", [128, 32], BF16)
        hbuf = sb("hbuf", [128, DC, T], F32)
        zbuf = sb("zbuf", [128, DC, T], F32)
        ab = sb("ab", [128, DC, T], BF16)
        hid = sb("hid", [128, FH, T], BF16)
        gb = sb("gb", [128, BC, HB + T], BF16)
        dg = sb("dg", [128, 2, KB, 128], BF16)
        xs = sb("xs", [128, NXS, T], F32)
        sq = sb("sq", [128, NSQ, T], BF16)
        wsl = sb("wsl", [128, NSLOT, KS * 128], BF16)
        cvw = sb("cvw", [128, 2, T + KA - 1], F32)
        cvh = sb("cvh", [128, AC, KA - 1], F32)
        ft = sb("ft", [128, NF, T], F32)
        rX = sb("rX", [128, T], F32)
        nmr = sb("nmr", [128, T], F32)
        qe = sb("qe", [128, T], F32)
        ps = [es.enter_context(nc.psum_tensor(f"ps{b}", [128, T], F32)) for b in range(8)]
        hidF = hid.bitcast(F32)

        def G(c):
            return bass.AP(hidF, MC * (T // 2) + c * T, [[FH * (T // 2), 128], [1, T]])

        def Gkeys(c):
            return [("hid", MC + 2 * c), ("hid", MC + 2 * c + 1)]

        def pcol(off):
            return P[:, off:off + 1]

        semnames = ["pe", "act", "dve", "pool", "sp", "prm"] + [f"w{i}" for i in range(NSLOT)] \
            + [f"xs{i}" for i in range(NXS)] + [f"st{i}" for i in range(DC // 4)] + [f"hx{i}" for i in range(DC // 4)]
        sems = {n: es.enter_context(nc.semaphore(n)) for n in semnames}
        block = es.enter_context(nc.Block())

        S = Sched()
        S.const |= {("P",), ("onesD",), ("onesB",), ("ident",), ("eps",), ("smat",)}
        SA, SB = 6, 7
        state = {"bank": 0, "xs": 0, "sq": 0, "ft": 0, "wl": 0, "wu": 0}
        sq_pending = [False] * NSQ

        def bank():
            b = state["bank"]
            state["bank"] = (b + 1) % 6
            return b

        def xs_next():
            k = state["xs"]
            state["xs"] = (k + 1) % NXS
            return k

        def sq_next():
            q = state["sq"]
            state["sq"] = (q + 1) % NSQ
            assert not sq_pending[q], "sq slot reused before its consumer was emitted"
            sq_pending[q] = True
            return q

        def ft_next():
            f = state["ft"]
            state["ft"] = (f + 1) % NF
            return f

        def stat(bk, ones, ones_key, q, first, last):
            S.stat_mm(ps[bk][:, :], ("ps", bk), ones[:, :], sq[:, q, :], first, last, [("sq", q), ones_key])
            sq_pending[q] = False

        total_loads = NU * NT

        def w_issue():
            l = state["wl"]
            if l >= total_loads:
                return
            slot, u = l % NSLOT, l % NU
            S.dma("pool", _dma(wsl[:, slot, :], wall[u, :, :]), f"w{slot}", writes=[("w", slot)])
            state["wl"] = l + 1

        def w_take(n):
            l = state["wu"]
            state["wu"] = l + n
            assert state["wu"] <= state["wl"]
            return [(l + i) % NSLOT for i in range(n)]

        def wgroup(bk, nk, rhs_of, rkey_of):
            slots = w_take(nk // KS)
            items = []
            for kk in range(nk):
                s = slots[kk // KS]
                o = (kk % KS) * 128
                items.append((wsl[:, s, o:o + 128], rhs_of(kk), [("w", s), rkey_of(kk)]))
            tok = S.mm_group(ps[bk][:, :], ("ps", bk), items)
            for _ in slots:
                w_issue()
            return tok

        def ld_x(j, c, k):
            S.dma("sp", _dma(xs[:, k, :], xT[c * 128:(c + 1) * 128, j * T:(j + 1) * T]), f"xs{k}",
                  writes=[("xs", k)])

        xT_v = xT.rearrange("(c p) t -> p c t", p=128)
        S.dma("sp", _dma(P[:, :], prm[:, :]), "prm", writes=[("P",)])
        S.op("dve", _memset(onesD[:, :], 1.0 / D), writes=[("onesD",)])
        S.op("dve", _memset(onesB[:, :], 1.0 / cfg.DB), writes=[("onesB",)])
        S.op("dve", _memset(epsT[:, :], EPS), writes=[("eps",)])
        for c in range(BC):
            S.op("dve", _memset(gb[:, c, 0:HB], 0.0), writes=[("gbh", c)])
        for c in range(AC):
            S.op("dve", _memset(cvh[:, c, :], 0.0), writes=[("cvh", c)])

        for _ in range(NSLOT):
            w_issue()

        def ld_h(j, g):
            S.dma("sp", _dma(hbuf[:, 4 * g:4 * g + 4, :], xT_v[:, 4 * g:4 * g + 4, j * T:(j + 1) * T]), f"hx{g}",
                  writes=[("hb", c) for c in range(4 * g, 4 * g + 4)])

        def prep_a_chunk(j, c):
            k = xs_next()
            ld_x(j, c, k)
            q = sq_next()
            S.op("act", _act(sq[:, q, :], xs[:, k, :], AF.Square), reads=[("xs", k)], writes=[("sq", q)])
            return q

        def prep_a_fin():
            f = ft_next()
            S.op("act", _act(ft[:, f, :], ps[SA][:, :], AF.Sqrt, bias=epsT[:, 0:1]),
                 reads=[("ps", SA), ("eps",)], writes=[("ft", f)])
            S.op("dve", _recip(rX[:, :], ft[:, f, :]), reads=[("ft", f)], writes=[("rX",)])

        def prep_b_chunk(j, c):
            k = xs_next()
            ld_x(j, c, k)
            S.op("dve", _stt(ab[:, c, :], xs[:, k, :], pcol(cfg.o_g1 + c), rX[:, :], ALU.mult, ALU.mult),
                 reads=[("xs", k), ("rX",), ("P",)], writes=[("ab", c)])

        def cdiv(a, b):
            return -(-a // b)

        def s1(j, epi):
            pend = []

            def tick():
                for p in pend:
                    p[0] -= 1
                while pend and pend[0][0] <= 0:
                    pend.pop(0)[1]()

            def flush():
                while pend:
                    tick()

            def conv_tiled(c):
                b = c % 2
                banks4 = [bank() for _ in range(4)]
                pe = S.E["pe"]
                rd = [("dg", b), ("gbc", c), ("gbh", c)]
                wr = [("ps", bk) for bk in banks4]
                n_total, cnt, tok = KB * 4, 0, None
                for r in range(cdiv(KB, 4)):
                    for jq in range(4):
                        k = 4 * r + jq
                        if k >= KB:
                            continue
                        for i in range(4):
                            cnt += 1
                            waits = S.deps(rd, wr) if cnt == 1 else []
                            fin = cnt == n_total
                            t = pe.add(_mmt(ps[banks4[i]][32 * jq:32 * jq + 32, :],
                                            dg[32 * i:32 * i + 32, b, k, 32 * i:32 * i + 32],
                                            gb[32 * i:32 * i + 32, c, k:k + T],
                                            r == 0, k + 4 >= KB, (32 * i, 32 * jq)), waits, inc=fin)
                            if fin:
                                tok = t
                S.commit(tok, rd, wr)
                qs = []
                for i in range(4):
                    q = sq_next()
                    S.op("act", _act(sq[:, q, :], ps[banks4[i]][:, :], AF.Identity), reads=[("ps", banks4[i])], writes=[("sq", q)])
                    qs.append(q)
                bR = bank()
                rtok = None
                for i in range(4):
                    rdq = [("sq", qs[i]), ("smat",)]
                    waits = S.deps(rdq, [("ps", bR)] if i == 0 else [])
                    rtok = pe.add(_mmt(ps[bR][32 * i:32 * i + 32, :], smat[:, :], sq[:, qs[i], :], True, True, (0, 32 * i)),
                                  waits, inc=(i == 3))
                S.commit(rtok, [("sq", q) for q in qs], [("ps", bR)])
                for q in qs:
                    sq_pending[q] = False
                return bR

            def conv(c):
                b = c % 2
                if cfg.CONV_TILED:
                    bk = conv_tiled(c)
                else:
                    bk = bank()
                    items = [(dg[:, b, k, :], gb[:, c, k:k + T], [("dg", b), ("gbc", c), ("gbh", c)]) for k in range(KB)]
                    S.mm_group(ps[bk][:, :], ("ps", bk), items)
                S.op("dve", _copy(gb[:, c, 0:HB], gb[:, c, T:T + HB]), reads=[("gbc", c)], writes=[("gbh", c)])
                S.op("act", _act(G(c), ps[bk][:, :], AF.Identity, bias=pcol(cfg.o_bb + c)),
                     reads=[("ps", bk), ("P",)], writes=Gkeys(c))
                q1 = sq_next()
                S.op("act", _act(sq[:, q1, :], G(c), AF.Identity), reads=Gkeys(c), writes=[("sq", q1)])
                q2 = sq_next()
                S.op("act", _act(sq[:, q2, :], G(c), AF.Square), reads=Gkeys(c), writes=[("sq", q2)])

                def st():
                    stat(SA, onesB, ("onesB",), q1, c == 0, c == BC - 1)
                    stat(SB, onesB, ("onesB",), q2, c == 0, c == BC - 1)
                pend.insert(0, [1, st])

            def lnfin():
                fm = ft_next()
                S.op("act", _act(ft[:, fm, :], ps[SA][:, :], AF.Identity), reads=[("ps", SA)], writes=[("ft", fm)])
                f2 = ft_next()
                S.op("dve", _tt(ft[:, f2, :], ft[:, fm, :], ft[:, fm, :], ALU.mult), reads=[("ft", fm)], writes=[("ft", f2)])
                f3 = ft_next()
                S.op("dve", _tt(ft[:, f3, :], ps[SB][:, :], ft[:, f2, :], ALU.subtract),
                     reads=[("ps", SB), ("ft", f2)], writes=[("ft", f3)])
                f4 = ft_next()
                S.op("act", _act(ft[:, f4, :], ft[:, f3, :], AF.Sqrt, bias=epsT[:, 0:1]),
                     reads=[("ft", f3), ("eps",)], writes=[("ft", f4)])
                S.op("dve", _recip(rX[:, :], ft[:, f4, :]), reads=[("ft", f4)], writes=[("rX",)])
                S.op("dve", _stt(nmr[:, :], ft[:, fm, :], -1.0, rX[:, :], ALU.mult, ALU.mult),
                     reads=[("ft", fm), ("rX",)], writes=[("nmr",)])

            def lnapply(c):
                f = ft_next()
                S.op("dve", _tt(ft[:, f, :], G(c), rX[:, :], ALU.mult), reads=Gkeys(c) + [("rX",)], writes=[("ft", f)])
                S.op("dve", _tt(ft[:, f, :], ft[:, f, :], nmr[:, :], ALU.add), reads=[("ft", f), ("nmr",)], writes=[("ft", f)])
                S.op("act", _act(hid[:, AC + c, :], ft[:, f, :], AF.Silu, bias=pcol(cfg.o_lb + c), scale=pcol(cfg.o_lg + c)),
                     reads=[("ft", f), ("P",)], writes=[("hid", AC + c)])

            for c in range(BC):
                bv, bg = bank(), bank()
                wgroup(bv, DC, lambda kk: ab[:, kk, :], lambda kk: ("ab", kk))
                wgroup(bg, DC, lambda kk: ab[:, kk, :], lambda kk: ("ab", kk))
                f = ft_next()
                S.op("act", _act(ft[:, f, :], ps[bg][:, :], AF.Sigmoid), reads=[("ps", bg)], writes=[("ft", f)])
                S.op("dve", _tt(gb[:, c, HB:HB + T], ps[bv][:, :], ft[:, f, :], ALU.mult),
                     reads=[("ps", bv), ("ft", f)], writes=[("gbc", c)])
                b = c % 2
                wbc = P[:, cfg.o_wb + c * KB: cfg.o_wb + (c + 1) * KB].unsqueeze(2).broadcast_to([128, KB, 128])
                idc = ident[:, :].unsqueeze(1).broadcast_to([128, KB, 128])
                S.op("dve", _tt(dg[:, b, :, :], idc, wbc, ALU.mult), reads=[("ident",), ("P",)], writes=[("dg", b)])
                pend.append([2, (lambda c=c: conv(c))])
                tick()
                for _ in range(cdiv(len(epi), BC - c)):
                    epi.pop(0)()

            ln = {"fin": False, "next": 0}
            for i in range(AC):
                bC, bV, bB = bank(), bank(), bank()
                wgroup(bC, DC, lambda kk: ab[:, kk, :], lambda kk: ("ab", kk))
                wgroup(bV, DC, lambda kk: ab[:, kk, :], lambda kk: ("ab", kk))
                wgroup(bB, DC, lambda kk: ab[:, kk, :], lambda kk: ("ab", kk))
                c = i
                f1 = ft_next()
                S.op("act", _act(ft[:, f1, :], ps[bC][:, :], AF.Identity), reads=[("ps", bC)], writes=[("ft", f1)])
                w = c % 2
                S.op("dve", _copy(cvw[:, w, 0:KA - 1], cvh[:, c, :]), reads=[("cvh", c)], writes=[("cvw", w)])
                S.op("dve", _tt(cvw[:, w, KA - 1:KA - 1 + T], ps[bV][:, :], ft[:, f1, :], ALU.mult),
                     reads=[("ps", bV), ("ft", f1)], writes=[("cvw", w)])
                f2 = ft_next()
                S.op("dve", _ts(ft[:, f2, :], cvw[:, w, KA - 1:KA - 1 + T], pcol(cfg.o_wa + c * KA + KA - 1), None, ALU.mult),
                     reads=[("cvw", w), ("P",)], writes=[("ft", f2)])
                for k in range(KA - 2, -1, -1):
                    S.op("dve", _stt(ft[:, f2, :], cvw[:, w, k:k + T], pcol(cfg.o_wa + c * KA + k), ft[:, f2, :], ALU.mult, ALU.add),
                         reads=[("cvw", w), ("ft", f2), ("P",)], writes=[("ft", f2)])
                S.op("dve", _tt(hid[:, c, :], ps[bB][:, :], ft[:, f2, :], ALU.mult),
                     reads=[("ps", bB), ("ft", f2)], writes=[("hid", c)])
                S.op("dve", _copy(cvh[:, c, :], cvw[:, w, T:T + KA - 1]), reads=[("cvw", w)], writes=[("cvh", c)])
                tick()
                for _ in range(cdiv(len(epi), AC - i)):
                    epi.pop(0)()
                if not ln["fin"]:
                    if not pend:
                        lnfin()
                        ln["fin"] = True
                else:
                    for _ in range(cdiv(BC - ln["next"], max(1, AC - 2 - i))):
                        if ln["next"] < BC:
                            lnapply(ln["next"])
                            ln["next"] += 1
            flush()
            while epi:
                epi.pop(0)()
            if not ln["fin"]:
                lnfin()
            while ln["next"] < BC:
                lnapply(ln["next"])
                ln["next"] += 1
            S.op("act", _act(dummy[:, 1:2], epsT[:, 0:1], AF.Sqrt), reads=[("eps",)], writes=[("dummy",)])

        def s2(j):
            pend = None
            for n in range(DC):
                bk = bank()
                wgroup(bk, MC, lambda kk: hid[:, kk, :], lambda kk: ("hid", kk))
                if pend is not None:
                    pend()
                S.op("act", _act(zbuf[:, n, :], ps[bk][:, :], AF.Identity, scale=pcol(cfg.o_g2 + n)),
                     reads=[("ps", bk), ("P",)], writes=[("zb", n)])
                q = sq_next()
                S.op("act", _act(sq[:, q, :], ps[bk][:, :], AF.Square), reads=[("ps", bk)], writes=[("sq", q)])
                pend = (lambda q=q, n=n: stat(SA, onesD, ("onesD",), q, n == 0, n == DC - 1))
            pend()
            f = ft_next()
            S.op("act", _act(ft[:, f, :], ps[SA][:, :], AF.Sqrt, bias=epsT[:, 0:1]),
                 reads=[("ps", SA), ("eps",)], writes=[("ft", f)])
            S.op("dve", _recip(rX[:, :], ft[:, f, :]), reads=[("ft", f)], writes=[("rX",)])
            for c in range(DC):
                e = "dve"
                S.op(e, _tt(zbuf[:, c, :], zbuf[:, c, :], rX[:, :], ALU.mult),
                     reads=[("zb", c), ("rX",)], writes=[("zb", c)])
                S.op(e, _tt(hbuf[:, c, :], zbuf[:, c, :], hbuf[:, c, :], ALU.add),
                     reads=[("zb", c), ("hb", c)], writes=[("hb", c)])
                S.op("act", _act(ab[:, c, :], hbuf[:, c, :], AF.Identity, scale=pcol(cfg.o_g3 + c)),
                     reads=[("hb", c), ("P",)], writes=[("ab", c)])

        NG3 = min(5, FH, DC)

        def s3_first(ng):
            banks = [bank() for _ in range(ng)]
            slots = [w_take(DC // KS) for _ in range(ng)]
            pe = S.E["pe"]
            toks = [None] * ng
            reads = [[] for _ in range(ng)]
            for kk in range(DC):
                for g in range(ng):
                    sl = slots[g][kk // KS]
                    o = (kk % KS) * 128
                    rd = [("w", sl), ("ab", kk)]
                    bk = banks[g]
                    waits = S.deps(rd, [("ps", bk)] if kk == 0 else [])
                    last = kk == DC - 1
                    t = pe.add(_mm(ps[bk][:, :], wsl[:, sl, o:o + 128], ab[:, kk, :], kk == 0, last), waits, inc=last)
                    reads[g] += rd
                    if last:
                        toks[g] = t
            for g in range(ng):
                S.commit(toks[g], reads[g], [("ps", banks[g])])
            for g in range(ng):
                for _ in slots[g]:
                    w_issue()
            for g in range(ng):
                fb = g
                qh = sq_next()
                S.op("act", _act(sq[:, qh, :], hbuf[:, fb, :], AF.Square), reads=[("hb", fb)], writes=[("sq", qh)])
                f = ft_next()
                S.op("act", _act(ft[:, f, :], ps[banks[g]][:, :], AF.Relu), reads=[("ps", banks[g])], writes=[("ft", f)])
                S.op("dve", _tt(hid[:, fb, :], ft[:, f, :], ft[:, f, :], ALU.mult), reads=[("ft", f)], writes=[("hid", fb)])
                stat(SB, onesD, ("onesD",), qh, fb == 0, fb == DC - 1)
                if fb == DC - 1:
                    S.op("act", _act(qe[:, :], ps[SB][:, :], AF.Square, bias=epsT[:, 0:1]),
                         reads=[("ps", SB), ("eps",)], writes=[("qe",)])

        def s34(j):
            nxt = j + 1 < NT
            for half in range(2):
                fb0 = 0
                if half == 0:
                    fb0 = NG3
                    s3_first(NG3)
                for fb in range(fb0, FH):
                    qh = None
                    if half == 0 and fb < DC:
                        qh = sq_next()
                        S.op("act", _act(sq[:, qh, :], hbuf[:, fb, :], AF.Square), reads=[("hb", fb)], writes=[("sq", qh)])
                    bk = bank()
                    wgroup(bk, DC, lambda kk: ab[:, kk, :], lambda kk: ("ab", kk))
                    if qh is not None:
                        stat(SB, onesD, ("onesD",), qh, fb == 0, fb == DC - 1)
                        if fb == DC - 1:
                            S.op("act", _act(qe[:, :], ps[SB][:, :], AF.Square, bias=epsT[:, 0:1]),
                                 reads=[("ps", SB), ("eps",)], writes=[("qe",)])
                    f = ft_next()
                    S.op("act", _act(ft[:, f, :], ps[bk][:, :], AF.Relu), reads=[("ps", bk)], writes=[("ft", f)])
                    S.op("dve", _tt(hid[:, fb, :], ft[:, f, :], ft[:, f, :], ALU.mult), reads=[("ft", f)], writes=[("hid", fb)])
                pend = None
                for n in range(DC):
                    qx = None
                    if nxt and half == 0:
                        qx = prep_a_chunk(j + 1, n)
                    if nxt and half == 1:
                        prep_b_chunk(j + 1, n)
                    bk = bank()
                    wgroup(bk, FH, lambda kk: hid[:, kk, :], lambda kk: ("hid", kk))
                    if qx is not None:
                        stat(SA, onesD, ("onesD",), qx, n == 0, n == DC - 1)
                    if pend is not None:
                        pend()
                        pend = None
                    if half == 0:
                        S.op("act", _act(zbuf[:, n, :], ps[bk][:, :], AF.Identity), reads=[("ps", bk)], writes=[("zb", n)])
                    else:
                        S.op("dve", _tt(zbuf[:, n, :], ps[bk][:, :], zbuf[:, n, :], ALU.add),
                             reads=[("ps", bk), ("zb", n)], writes=[("zb", n)])
                        q = sq_next()
                        S.op("act", _act(sq[:, q, :], zbuf[:, n, :], AF.Square), reads=[("zb", n)], writes=[("sq", q)])
                        pend = (lambda q=q, n=n: stat(SB, onesD, ("onesD",), q, n == 0, n == DC - 1))
                if pend is not None:
                    pend()
                if half == 0 and nxt:
                    prep_a_fin()

        def epilogue_head(j):
            f = ft_next()
            S.op("dve", _stt(ft[:, f, :], qe[:, :], EPS, ps[SB][:, :], ALU.mult, ALU.add),
                 reads=[("qe",), ("ps", SB)], writes=[("ft", f)])
            f2 = ft_next()
            S.op("act", _act(ft[:, f2, :], ft[:, f, :], AF.Sqrt), reads=[("ft", f)], writes=[("ft", f2)])
            S.op("dve", _recip(qe[:, :], ft[:, f2, :]), reads=[("ft", f2)], writes=[("qe",)])

        def epilogue_chunks(j, engines):
            def mk(c):
                def fn():
                    e = engines[c % len(engines)]
                    S.op(e, _stt(zbuf[:, c, :], zbuf[:, c, :], pcol(cfg.o_g4 + c), qe[:, :], ALU.mult, ALU.mult),
                         reads=[("zb", c), ("qe",), ("P",)], writes=[("zb", c)])
                    S.op(e, _tt(zbuf[:, c, :], zbuf[:, c, :], hbuf[:, c, :], ALU.add),
                         reads=[("zb", c), ("hb", c)], writes=[("zb", c)])
                    if c % 4 == 3:
                        g = c // 4
                        S.dma("sp", _dma(outT_v[:, 4 * g:4 * g + 4, j * T:(j + 1) * T], zbuf[:, 4 * g:4 * g + 4, :]),
                              f"st{g}", reads=[("zb", cc) for cc in range(4 * g, 4 * g + 4)])
                        if j + 1 < NT:
                            ld_h(j + 1, g)
                return fn
            return [mk(c) for c in range(DC)]

        build_ident(S, es, nc, ident, smat)

        for g in range(DC // 4):
            ld_h(0, g)
        for c in range(DC):
            q = sq_next()
            S.op("act", _act(sq[:, q, :], hbuf[:, c, :], AF.Square), reads=[("hb", c)], writes=[("sq", q)])
            stat(SA, onesD, ("onesD",), q, c == 0, c == DC - 1)
        prep_a_fin()
        for c in range(DC):
            S.op("dve", _stt(ab[:, c, :], hbuf[:, c, :], pcol(cfg.o_g1 + c), rX[:, :], ALU.mult, ALU.mult),
                 reads=[("hb", c), ("rX",), ("P",)], writes=[("ab", c)])
        epi = []
        for j in range(NT):
            s1(j, epi)
            s2(j)
            s34(j)
            epilogue_head(j)
            epi = epilogue_chunks(j, ["dve"])
        while epi:
            epi.pop(0)()

        def replay(stream, h, final_waits=()):
            waited = {}
            for waits, fn, inc, dsem in stream.ops:
                mx = {}
                for k, v in waits:
                    if mx.get(k, 0) < v:
                        mx[k] = v
                for k, v in mx.items():
                    if waited.get(k, 0) >= v:
                        continue
                    h.wait_ge(sems[k], v)
                    waited[k] = v
                inst = fn(h)
                if inc:
                    inst.then_inc(sems[stream.name], 1)
                elif dsem is not None:
                    inst.then_inc(sems[dsem], 16)
            for k, v in final_waits:
                h.wait_ge(sems[k], v)

        @block.tensor
        def _(h):
            replay(S.E["pe"], h)

        @block.scalar
        def _(h):
            replay(S.E["act"], h)

        @block.vector
        def _(h):
            replay(S.E["dve"], h)

        @block.gpsimd
        def _(h):
            replay(S.E["pool"], h)

        @block.sync
        def _(h):
            fin = [(f"st{g}", 16 * S.dcount[f"st{g}"]) for g in range(DC // 4)]
            replay(S.E["sp"], h, fin)

    return nc


def build_ident(S, es, nc, ident, smat):
    I32 = mybir.dt.int32
    io_f = es.enter_context(nc.sbuf_tensor("io_f", [128, 128], I32))
    io_p = es.enter_context(nc.sbuf_tensor("io_p", [128, 1], I32))
    io_pf = es.enter_context(nc.sbuf_tensor("io_pf", [128, 1], F32))
    S.op("pool", lambda h: h.iota(io_f[:, :], [[1, 128]], base=0, channel_multiplier=0), writes=[("io_f",)])
    S.op("pool", lambda h: h.iota(io_p[:, :], [[1, 1]], base=0, channel_multiplier=1), writes=[("io_p",)])
    S.op("dve", _copy(io_pf[:, :], io_p[:, :]), reads=[("io_p",)], writes=[("io_pf",)])
    S.op("dve", _ts(ident[:, :], io_f[:, :], io_pf[:, 0:1], None, ALU.is_equal),
         reads=[("io_f",), ("io_pf",)], writes=[("ident",)])
    for q in range(4):
        S.op("dve", _copy(smat[32 * q:32 * q + 32, :], ident[32 * q:32 * q + 32, 32 * q:32 * q + 32]),
             reads=[("ident",)], writes=[("smat",)])


_FULL = Cfg()


def kernel(x, mix_pre_gain, w_in, conv_a_w, conv_b_w, conv_b_bias, ln_b_gain, ln_b_bias, w_out,
           mix_post_gain, mlp_pre_gain, w_up, w_down, mlp_post_gain):
    cfg = _FULL
    f = lambda a: np.asarray(a, dtype=np.float32)
    x = f(x)
    nb = x.shape[0]
    prm = pack_params(cfg, f(mix_pre_gain)[0], f(mix_post_gain)[0], f(mlp_pre_gain)[0], f(mlp_post_gain)[0],
                      f(conv_a_w)[0], f(conv_b_w)[0], f(conv_b_bias)[0], f(ln_b_gain)[0], f(ln_b_bias)[0])
    wall = pack_weights(cfg, f(w_in)[0], f(w_out)[0], f(w_up)[0], f(w_down)[0])
    nc = build_program(cfg)
    in_maps = [{"xT": np.ascontiguousarray(x[b].T), "wall": wall, "prm": prm} for b in range(nb)]
    res = run_bass_kernel_spmd(nc, in_maps, core_ids=list(range(nb)))
    out = np.empty_like(x)
    for b in range(nb):
        out[b] = res.results[b]["outT"].T
    return out
```

```python
from contextlib import ExitStack

import numpy as np
import concourse.bass as bass
import concourse.mybir as mybir
from concourse.bass_utils import run_bass_kernel_spmd

F32 = mybir.dt.float32
BF16 = mybir.dt.bfloat16
ALU = mybir.AluOpType
AF = mybir.ActivationFunctionType


class Cfg:
    def __init__(self, D=2048, DA=1024, DB=1024, DFF=8192, S=2048, T=512, KS=16, KA=3, KB=31,
                 NSLOT=8, NXS=3, NSQ=6, NF=6, EPS=1e-6, CONV_TILED=True):
        self.CONV_TILED = CONV_TILED
        self.D, self.DA, self.DB, self.DFF, self.S, self.T = D, DA, DB, DFF, S, T
        self.KS, self.KA, self.KB, self.NSLOT, self.NXS, self.NSQ, self.NF, self.EPS = KS, KA, KB, NSLOT, NXS, NSQ, NF, EPS
        self.DC, self.AC, self.BC, self.FC = D // 128, DA // 128, DB // 128, DFF // 128
        self.MC = self.AC + self.BC
        self.DIN = 3 * DA + 2 * DB
        self.NT = S // T
        self.FH = self.FC // 2
        self.HB = KB - 1
        assert self.DC % KS == 0 and self.MC % KS == 0 and self.FH % KS == 0
        assert self.FH >= self.MC + 2 * self.BC and self.DC % 4 == 0
        self.NU = self.DIN // 128 * (self.DC // KS) + self.DC * (self.MC // KS) \
            + 2 * (self.FH * (self.DC // KS) + self.DC * (self.FH // KS))
        DC, AC, BC = self.DC, self.AC, self.BC
        self.o_g1, self.o_g2, self.o_g3, self.o_g4 = 0, DC, 2 * DC, 3 * DC
        self.o_wa = 4 * DC
        self.o_wb = self.o_wa + AC * KA
        self.o_bb = self.o_wb + BC * KB
        self.o_lg = self.o_bb + BC
        self.o_lb = self.o_lg + BC
        self.NP = self.o_lb + BC


def _chunkvec(v):
    return np.ascontiguousarray(v.reshape(-1, 128).T)


def pack_params(cfg, g1, g2, g3, g4, wa, wb, bb, lg, lb):
    cols = [_chunkvec(g1), _chunkvec(g2), _chunkvec(g3), _chunkvec(g4),
            wa.T.reshape(cfg.AC, 128, cfg.KA).transpose(1, 0, 2).reshape(128, -1),
            wb.T.reshape(cfg.BC, 128, cfg.KB).transpose(1, 0, 2).reshape(128, -1),
            _chunkvec(bb), _chunkvec(lg), _chunkvec(lb)]
    p = np.ascontiguousarray(np.concatenate(cols, axis=1).astype(np.float32))
    assert p.shape == (128, cfg.NP)
    return p


def pack_weights(cfg, w_in, w_out, w_up, w_down):
    KS = cfg.KS
    wall = np.empty((cfg.NU, 128, KS * 128), np.float32)
    u = 0

    def put(W, k0, nb):
        nonlocal u
        blk = W[k0 * 128:(k0 + KS) * 128, nb * 128:(nb + 1) * 128]
        wall[u] = blk.reshape(KS, 128, 128).transpose(1, 0, 2).reshape(128, KS * 128)
        u += 1

    DA, DB = cfg.DA, cfg.DB
    for m in s1_colblocks(cfg):
        for s in range(cfg.DC // KS):
            put(w_in, s * KS, m)
    for n in range(cfg.DC):
        for s in range(cfg.MC // KS):
            put(w_out, s * KS, n)
    for half in range(2):
        for fb in range(cfg.FH):
            for s in range(cfg.DC // KS):
                put(w_up, s * KS, half * cfg.FH + fb)
        for n in range(cfg.DC):
            for s in range(cfg.FH // KS):
                put(w_down, half * cfg.FH + s * KS, n)
    assert u == cfg.NU
    return wall


def s1_colblocks(cfg):
    AC, BC = cfg.AC, cfg.BC
    out = []
    for c in range(BC):
        out += [3 * AC + c, 3 * AC + BC + c]
    for c in range(AC):
        out += [AC + c, 2 * AC + c, c]
    return out


class EngStream:
    def __init__(self, name):
        self.name, self.ops, self.count = name, [], 0

    def add(self, fn, waits, inc=True):
        if inc:
            self.count += 1
        self.ops.append((tuple(waits), fn, inc, None))
        return (self.name, self.count) if inc else None

    def add_dma(self, fn, waits, sem):
        self.ops.append((tuple(waits), fn, False, sem))


class Sched:
    def __init__(self):
        self.E = {n: EngStream(n) for n in ("pe", "act", "dve", "pool", "sp")}
        self.track = {}
        self.const = set()
        self.dcount = {}

    def deps(self, reads, writes):
        w = []
        for k in reads:
            st = self.track.get(k)
            if st and st[0] is not None:
                w.append(st[0])
        for k in writes:
            st = self.track.get(k)
            if st:
                if st[0] is not None:
                    w.append(st[0])
                w.extend(st[1].items())
        return w

    def commit(self, tok, reads, writes):
        for k in reads:
            if k in self.const:
                continue
            st = self.track.setdefault(k, [None, {}])
            if st[1].get(tok[0], 0) < tok[1]:
                st[1][tok[0]] = tok[1]
        for k in writes:
            self.track[k] = [tok, {}]

    def op(self, eng, fn, reads=(), writes=()):
        tok = self.E[eng].add(fn, self.deps(reads, writes), True)
        self.commit(tok, reads, writes)
        return tok

    def dma(self, eng, fn, sem, reads=(), writes=()):
        n = self.dcount.get(sem, 0) + 1
        self.dcount[sem] = n
        self.E[eng].add_dma(fn, self.deps(reads, writes), sem)
        tok = (sem, 16 * n)
        self.commit(tok, reads, writes)
        return tok

    def mm_group(self, out_ap, bank_key, items):
        pe = self.E["pe"]
        n = len(items)
        allreads = []
        tok = None
        for i, (lhsT, rhs, reads) in enumerate(items):
            waits = self.deps(reads, [bank_key] if i == 0 else [])
            last = i == n - 1
            tok = pe.add(_mm(out_ap, lhsT, rhs, i == 0, last), waits, inc=last)
            allreads += list(reads)
        self.commit(tok, allreads, [bank_key])
        return tok

    def stat_mm(self, out_ap, bank_key, lhsT, rhs, first, last, reads):
        waits = self.deps(reads, [bank_key] if first else [])
        tok = self.E["pe"].add(_mm(out_ap, lhsT, rhs, first, last), waits, inc=True)
        self.commit(tok, reads, [bank_key])
        return tok


def _mm(out_ap, lhsT, rhs, start, stop):
    return lambda h: h.matmul(out_ap, lhsT, rhs, start=start, stop=stop)


def _mmt(out_ap, lhsT, rhs, start, stop, tp):
    return lambda h: h.matmul(out_ap, lhsT, rhs, start=start, stop=stop, tile_position=tp)


def _act(out, in_, func, bias=None, scale=None):
    kw = {}
    if bias is not None:
        kw["bias"] = bias
    if scale is not None:
        kw["scale"] = scale
    return lambda h: h.activation(out=out, in_=in_, func=func, **kw)


def _tt(out, in0, in1, op):
    return lambda h: h.tensor_tensor(out=out, in0=in0, in1=in1, op=op)


def _stt(out, in0, scalar, in1, op0, op1):
    return lambda h: h.scalar_tensor_tensor(out=out, in0=in0, scalar=scalar, in1=in1, op0=op0, op1=op1)


def _ts(out, in0, s1, s2, op0, op1=None):
    if op1 is None:
        return lambda h: h.tensor_scalar(out=out, in0=in0, scalar1=s1, scalar2=None, op0=op0)
    return lambda h: h.tensor_scalar(out=out, in0=in0, scalar1=s1, scalar2=s2, op0=op0, op1=op1)


def _copy(out, in_):
    return lambda h: h.tensor_copy(out, in_)


def _recip(out, in_):
    return lambda h: h.reciprocal(out=out, in_=in_)


def _memset(ap, v):
    return lambda h: h.memset(ap, v)


def _dma(out, in_):
    return lambda h: h.dma_start(out=out, in_=in_)


def build_program(cfg):
    D, S_, T = cfg.D, cfg.S, cfg.T
    DC, AC, BC, MC, FH, KS, KA, KB, HB = cfg.DC, cfg.AC, cfg.BC, cfg.MC, cfg.FH, cfg.KS, cfg.KA, cfg.KB, cfg.HB
    NSLOT, NXS, NSQ, NF, EPS, NT, NP, NU = cfg.NSLOT, cfg.NXS, cfg.NSQ, cfg.NF, cfg.EPS, cfg.NT, cfg.NP, cfg.NU

    nc = bass.Bass("TRN2", target_bir_lowering=False)
    xT = nc.dram_tensor("xT", [D, S_], F32, kind="ExternalInput")
    wall = nc.dram_tensor("wall", [NU, 128, KS * 128], F32, kind="ExternalInput")
    prm = nc.dram_tensor("prm", [128, NP], F32, kind="ExternalInput")
    outT = nc.dram_tensor("outT", [D, S_], F32, kind="ExternalOutput")
    outT_v = outT.rearrange("(c p) t -> p c t", p=128)

    with ExitStack() as es:
        def sb(name, shape, dt):
            return es.enter_context(nc.sbuf_tensor(name, shape, dt))

        P = sb("P", [128, NP], F32)
        onesD = sb("onesD", [128, 128], BF16)
        onesB = sb("onesB", [128, 128], BF16)
        ident = sb("ident", [128, 128], BF16)
        epsT = sb("epsT", [128, 1], F32)
        dummy = sb("scr2", [128, 2], F32)
        smat = sb("smat", [128, 32], BF16)
        hbuf = sb("hbuf", [128, DC, T], F32)
        zbuf = sb("zbuf", [128, DC, T], F32)
        ab = sb("ab", [128, DC, T], BF16)
        hid = sb("hid", [128, FH, T], BF16)
        gb = sb("gb", [128, BC, HB + T], BF16)
        dg = sb("dg", [128, 2, KB, 128], BF16)
        xs = sb("xs", [128, NXS, T], F32)
        sq = sb("sq", [128, NSQ, T], BF16)
        wsl = sb("wsl", [128, NSLOT, KS * 128], BF16)
        cvw = sb("cvw", [128, 2, T + KA - 1], F32)
        cvh = sb("cvh", [128, AC, KA - 1], F32)
        ft = sb("ft", [128, NF, T], F32)
        rX = sb("rX", [128, T], F32)
        nmr = sb("nmr", [128, T], F32)
        qe = sb("qe", [128, T], F32)
        ps = [es.enter_context(nc.psum_tensor(f"ps{b}", [128, T], F32)) for b in range(8)]
        hidF = hid.bitcast(F32)

        def G(c):
            return bass.AP(hidF, MC * (T // 2) + c * T, [[FH * (T // 2), 128], [1, T]])

        def Gkeys(c):
            return [("hid", MC + 2 * c), ("hid", MC + 2 * c + 1)]

        def pcol(off):
            return P[:, off:off + 1]

        semnames = ["pe", "act", "dve", "pool", "sp", "prm"] + [f"w{i}" for i in range(NSLOT)] \
            + [f"xs{i}" for i in range(NXS)] + [f"st{i}" for i in range(DC // 4)] + [f"hx{i}" for i in range(DC // 4)]
        sems = {n: es.enter_context(nc.semaphore(n)) for n in semnames}
        block = es.enter_context(nc.Block())

        S = Sched()
        S.const |= {("P",), ("onesD",), ("onesB",), ("ident",), ("eps",), ("smat",)}
        SA, SB = 6, 7
        state = {"bank": 0, "xs": 0, "sq": 0, "ft": 0, "wl": 0, "wu": 0}
        sq_pending = [False] * NSQ

        def bank():
            b = state["bank"]
            state["bank"] = (b + 1) % 6
            return b

        def xs_next():
            k = state["xs"]
            state["xs"] = (k + 1) % NXS
            return k

        def sq_next():
            q = state["sq"]
            state["sq"] = (q + 1) % NSQ
            assert not sq_pending[q], "sq slot reused before its consumer was emitted"
            sq_pending[q] = True
            return q

        def ft_next():
            f = state["ft"]
            state["ft"] = (f + 1) % NF
            return f

        def stat(bk, ones, ones_key, q, first, last):
            S.stat_mm(ps[bk][:, :], ("ps", bk), ones[:, :], sq[:, q, :], first, last, [("sq", q), ones_key])
            sq_pending[q] = False

        total_loads = NU * NT

        def w_issue():
            l = state["wl"]
            if l >= total_loads:
                return
            slot, u = l % NSLOT, l % NU
            S.dma("pool", _dma(wsl[:, slot, :], wall[u, :, :]), f"w{slot}", writes=[("w", slot)])
            state["wl"] = l + 1

        def w_take(n):
            l = state["wu"]
            state["wu"] = l + n
            assert state["wu"] <= state["wl"]
            return [(l + i) % NSLOT for i in range(n)]

        def wgroup(bk, nk, rhs_of, rkey_of):
            slots = w_take(nk // KS)
            items = []
            for kk in range(nk):
                s = slots[kk // KS]
                o = (kk % KS) * 128
                items.append((wsl[:, s, o:o + 128], rhs_of(kk), [("w", s), rkey_of(kk)]))
            tok = S.mm_group(ps[bk][:, :], ("ps", bk), items)
            for _ in slots:
                w_issue()
            return tok

        def ld_x(j, c, k):
            S.dma("sp", _dma(xs[:, k, :], xT[c * 128:(c + 1) * 128, j * T:(j + 1) * T]), f"xs{k}",
                  writes=[("xs", k)])

        xT_v = xT.rearrange("(c p) t -> p c t", p=128)
        S.dma("sp", _dma(P[:, :], prm[:, :]), "prm", writes=[("P",)])
        S.op("dve", _memset(onesD[:, :], 1.0 / D), writes=[("onesD",)])
        S.op("dve", _memset(onesB[:, :], 1.0 / cfg.DB), writes=[("onesB",)])
        S.op("dve", _memset(epsT[:, :], EPS), writes=[("eps",)])
        for c in range(BC):
            S.op("dve", _memset(gb[:, c, 0:HB], 0.0), writes=[("gbh", c)])
        for c in range(AC):
            S.op("dve", _memset(cvh[:, c, :], 0.0), writes=[("cvh", c)])

        for _ in range(NSLOT):
            w_issue()

        def ld_h(j, g):
            S.dma("sp", _dma(hbuf[:, 4 * g:4 * g + 4, :], xT_v[:, 4 * g:4 * g + 4, j * T:(j + 1) * T]), f"hx{g}",
                  writes=[("hb", c) for c in range(4 * g, 4 * g + 4)])

        def prep_a_chunk(j, c):
            k = xs_next()
            ld_x(j, c, k)
            q = sq_next()
            S.op("act", _act(sq[:, q, :], xs[:, k, :], AF.Square), reads=[("xs", k)], writes=[("sq", q)])
            return q

        def prep_a_fin():
            f = ft_next()
            S.op("act", _act(ft[:, f, :], ps[SA][:, :], AF.Sqrt, bias=epsT[:, 0:1]),
                 reads=[("ps", SA), ("eps",)], writes=[("ft", f)])
            S.op("dve", _recip(rX[:, :], ft[:, f, :]), reads=[("ft", f)], writes=[("rX",)])

        def prep_b_chunk(j, c):
            k = xs_next()
            ld_x(j, c, k)
            S.op("dve", _stt(ab[:, c, :], xs[:, k, :], pcol(cfg.o_g1 + c), rX[:, :], ALU.mult, ALU.mult),
                 reads=[("xs", k), ("rX",), ("P",)], writes=[("ab", c)])

        def cdiv(a, b):
            return -(-a // b)

        def s1(j, epi):
            pend = []

            def tick():
                for p in pend:
                    p[0] -= 1
                ready = sorted([p for p in pend if p[0] <= 0], key=lambda p: p[1])
                for p in ready:
                    pend.remove(p)
                for p in ready:
                    p[2]()

            def flush():
                while pend:
                    tick()

            def conv_tiled(c):
                b = c % 2
                banks4 = [bank() for _ in range(4)]
                pe = S.E["pe"]
                rd = [("dg", b), ("gbc", c), ("gbh", c)]
                wr = [("ps", bk) for bk in banks4]
                n_total, cnt, tok = KB * 4, 0, None
                for r in range(cdiv(KB, 4)):
                    for jq in range(4):
                        k = 4 * r + jq
                        if k >= KB:
                            continue
                        for i in range(4):
                            cnt += 1
                            waits = S.deps(rd, wr) if cnt == 1 else []
                            fin = cnt == n_total
                            t = pe.add(_mmt(ps[banks4[i]][32 * jq:32 * jq + 32, :],
                                            dg[32 * i:32 * i + 32, b, k, 32 * i:32 * i + 32],
                                            gb[32 * i:32 * i + 32, c, k:k + T],
                                            r == 0, k + 4 >= KB, (32 * i, 32 * jq)), waits, inc=fin)
                            if fin:
                                tok = t
                S.commit(tok, rd, wr)
                qs = []
                for i in range(4):
                    q = sq_next()
                    S.op("act", _act(sq[:, q, :], ps[banks4[i]][:, :], AF.Identity), reads=[("ps", banks4[i])], writes=[("sq", q)])
                    qs.append(q)
                return qs

            def conv_reduce(c, qs):
                pe = S.E["pe"]
                bR = bank()
                rtok = None
                for i in range(4):
                    rdq = [("sq", qs[i]), ("smat",)]
                    waits = S.deps(rdq, [("ps", bR)] if i == 0 else [])
                    rtok = pe.add(_mmt(ps[bR][32 * i:32 * i + 32, :], smat[:, :], sq[:, qs[i], :], True, True, (0, 32 * i)),
                                  waits, inc=(i == 3))
                S.commit(rtok, [("sq", q) for q in qs], [("ps", bR)])
                for q in qs:
                    sq_pending[q] = False
                return bR

            def conv(c):
                b = c % 2
                if cfg.CONV_TILED:
                    qs = conv_tiled(c)
                    pend.append([1, 1, (lambda c=c, qs=qs: conv_post(c, conv_reduce(c, qs)))])
                    return
                else:
                    bk = bank()
                    items = [(dg[:, b, k, :], gb[:, c, k:k + T], [("dg", b), ("gbc", c), ("gbh", c)]) for k in range(KB)]
                    S.mm_group(ps[bk][:, :], ("ps", bk), items)
                conv_post(c, bk)

            def conv_post(c, bk):
                S.op("dve", _copy(gb[:, c, 0:HB], gb[:, c, T:T + HB]), reads=[("gbc", c)], writes=[("gbh", c)])
                S.op("act", _act(G(c), ps[bk][:, :], AF.Identity, bias=pcol(cfg.o_bb + c)),
                     reads=[("ps", bk), ("P",)], writes=Gkeys(c))
                q1 = sq_next()
                S.op("act", _act(sq[:, q1, :], G(c), AF.Identity), reads=Gkeys(c), writes=[("sq", q1)])
                q2 = sq_next()
                S.op("act", _act(sq[:, q2, :], G(c), AF.Square), reads=Gkeys(c), writes=[("sq", q2)])

                def st():
                    stat(SA, onesB, ("onesB",), q1, c == 0, c == BC - 1)
                    stat(SB, onesB, ("onesB",), q2, c == 0, c == BC - 1)
                pend.append([1, 0, st])

            def lnfin():
                fm = ft_next()
                S.op("act", _act(ft[:, fm, :], ps[SA][:, :], AF.Identity), reads=[("ps", SA)], writes=[("ft", fm)])
                f2 = ft_next()
                S.op("dve", _tt(ft[:, f2, :], ft[:, fm, :], ft[:, fm, :], ALU.mult), reads=[("ft", fm)], writes=[("ft", f2)])
                f3 = ft_next()
                S.op("dve", _tt(ft[:, f3, :], ps[SB][:, :], ft[:, f2, :], ALU.subtract),
                     reads=[("ps", SB), ("ft", f2)], writes=[("ft", f3)])
                f4 = ft_next()
                S.op("act", _act(ft[:, f4, :], ft[:, f3, :], AF.Sqrt, bias=epsT[:, 0:1]),
                     reads=[("ft", f3), ("eps",)], writes=[("ft", f4)])
                S.op("dve", _recip(rX[:, :], ft[:, f4, :]), reads=[("ft", f4)], writes=[("rX",)])
                S.op("dve", _stt(nmr[:, :], ft[:, fm, :], -1.0, rX[:, :], ALU.mult, ALU.mult),
                     reads=[("ft", fm), ("rX",)], writes=[("nmr",)])

            def lnapply(c):
                f = ft_next()
                S.op("dve", _tt(ft[:, f, :], G(c), rX[:, :], ALU.mult), reads=Gkeys(c) + [("rX",)], writes=[("ft", f)])
                S.op("dve", _tt(ft[:, f, :], ft[:, f, :], nmr[:, :], ALU.add), reads=[("ft", f), ("nmr",)], writes=[("ft", f)])
                S.op("act", _act(hid[:, AC + c, :], ft[:, f, :], AF.Silu, bias=pcol(cfg.o_lb + c), scale=pcol(cfg.o_lg + c)),
                     reads=[("ft", f), ("P",)], writes=[("hid", AC + c)])

            for c in range(BC):
                bv, bg = bank(), bank()
                wgroup(bv, DC, lambda kk: ab[:, kk, :], lambda kk: ("ab", kk))
                wgroup(bg, DC, lambda kk: ab[:, kk, :], lambda kk: ("ab", kk))
                f = ft_next()
                S.op("act", _act(ft[:, f, :], ps[bg][:, :], AF.Sigmoid), reads=[("ps", bg)], writes=[("ft", f)])
                S.op("dve", _tt(gb[:, c, HB:HB + T], ps[bv][:, :], ft[:, f, :], ALU.mult),
                     reads=[("ps", bv), ("ft", f)], writes=[("gbc", c)])
                b = c % 2
                wbc = P[:, cfg.o_wb + c * KB: cfg.o_wb + (c + 1) * KB].unsqueeze(2).broadcast_to([128, KB, 128])
                idc = ident[:, :].unsqueeze(1).broadcast_to([128, KB, 128])
                S.op("dve", _tt(dg[:, b, :, :], idc, wbc, ALU.mult), reads=[("ident",), ("P",)], writes=[("dg", b)])
                pend.append([2, 2, (lambda c=c: conv(c))])
                tick()
                for _ in range(cdiv(len(epi), BC - c)):
                    epi.pop(0)()

            ln = {"fin": False, "next": 0}
            for i in range(AC):
                bC, bV, bB = bank(), bank(), bank()
                wgroup(bC, DC, lambda kk: ab[:, kk, :], lambda kk: ("ab", kk))
                wgroup(bV, DC, lambda kk: ab[:, kk, :], lambda kk: ("ab", kk))
                wgroup(bB, DC, lambda kk: ab[:, kk, :], lambda kk: ("ab", kk))
                c = i
                f1 = ft_next()
                S.op("act", _act(ft[:, f1, :], ps[bC][:, :], AF.Identity), reads=[("ps", bC)], writes=[("ft", f1)])
                w = c % 2
                S.op("dve", _copy(cvw[:, w, 0:KA - 1], cvh[:, c, :]), reads=[("cvh", c)], writes=[("cvw", w)])
                S.op("dve", _tt(cvw[:, w, KA - 1:KA - 1 + T], ps[bV][:, :], ft[:, f1, :], ALU.mult),
                     reads=[("ps", bV), ("ft", f1)], writes=[("cvw", w)])
                f2 = ft_next()
                S.op("dve", _ts(ft[:, f2, :], cvw[:, w, KA - 1:KA - 1 + T], pcol(cfg.o_wa + c * KA + KA - 1), None, ALU.mult),
                     reads=[("cvw", w), ("P",)], writes=[("ft", f2)])
                for k in range(KA - 2, -1, -1):
                    S.op("dve", _stt(ft[:, f2, :], cvw[:, w, k:k + T], pcol(cfg.o_wa + c * KA + k), ft[:, f2, :], ALU.mult, ALU.add),
                         reads=[("cvw", w), ("ft", f2), ("P",)], writes=[("ft", f2)])
                S.op("dve", _tt(hid[:, c, :], ps[bB][:, :], ft[:, f2, :], ALU.mult),
                     reads=[("ps", bB), ("ft", f2)], writes=[("hid", c)])
                S.op("dve", _copy(cvh[:, c, :], cvw[:, w, T:T + KA - 1]), reads=[("cvw", w)], writes=[("cvh", c)])
                tick()
                for _ in range(cdiv(len(epi), AC - i)):
                    epi.pop(0)()
                if not ln["fin"]:
                    if not pend:
                        lnfin()
                        ln["fin"] = True
                else:
                    for _ in range(cdiv(BC - ln["next"], max(1, AC - 2 - i))):
                        if ln["next"] < BC:
                            lnapply(ln["next"])
                            ln["next"] += 1
            flush()
            while epi:
                epi.pop(0)()
            if not ln["fin"]:
                lnfin()
            while ln["next"] < BC:
                lnapply(ln["next"])
                ln["next"] += 1
            S.op("act", _act(dummy[:, 1:2], epsT[:, 0:1], AF.Sqrt), reads=[("eps",)], writes=[("dummy",)])

        def s2(j):
            pend = None
            for n in range(DC):
                bk = bank()
                wgroup(bk, MC, lambda kk: hid[:, kk, :], lambda kk: ("hid", kk))
                if pend is not None:
                    pend()
                S.op("act", _act(zbuf[:, n, :], ps[bk][:, :], AF.Identity, scale=pcol(cfg.o_g2 + n)),
                     reads=[("ps", bk), ("P",)], writes=[("zb", n)])
                q = sq_next()
                S.op("act", _act(sq[:, q, :], ps[bk][:, :], AF.Square), reads=[("ps", bk)], writes=[("sq", q)])
                pend = (lambda q=q, n=n: stat(SA, onesD, ("onesD",), q, n == 0, n == DC - 1))
            pend()
            f = ft_next()
            S.op("act", _act(ft[:, f, :], ps[SA][:, :], AF.Sqrt, bias=epsT[:, 0:1]),
                 reads=[("ps", SA), ("eps",)], writes=[("ft", f)])
            S.op("dve", _recip(rX[:, :], ft[:, f, :]), reads=[("ft", f)], writes=[("rX",)])
            for c in range(DC):
                e = "dve"
                S.op(e, _tt(zbuf[:, c, :], zbuf[:, c, :], rX[:, :], ALU.mult),
                     reads=[("zb", c), ("rX",)], writes=[("zb", c)])
                S.op(e, _tt(hbuf[:, c, :], zbuf[:, c, :], hbuf[:, c, :], ALU.add),
                     reads=[("zb", c), ("hb", c)], writes=[("hb", c)])
                S.op("act", _act(ab[:, c, :], hbuf[:, c, :], AF.Identity, scale=pcol(cfg.o_g3 + c)),
                     reads=[("hb", c), ("P",)], writes=[("ab", c)])

        NG3 = min(5, FH, DC)

        def s3_first(ng):
            banks = [bank() for _ in range(ng)]
            slots = [w_take(DC // KS) for _ in range(ng)]
            pe = S.E["pe"]
            toks = [None] * ng
            reads = [[] for _ in range(ng)]
            for kk in range(DC):
                for g in range(ng):
                    sl = slots[g][kk // KS]
                    o = (kk % KS) * 128
                    rd = [("w", sl), ("ab", kk)]
                    bk = banks[g]
                    waits = S.deps(rd, [("ps", bk)] if kk == 0 else [])
                    last = kk == DC - 1
                    t = pe.add(_mm(ps[bk][:, :], wsl[:, sl, o:o + 128], ab[:, kk, :], kk == 0, last), waits, inc=last)
                    reads[g] += rd
                    if last:
                        toks[g] = t
            for g in range(ng):
                S.commit(toks[g], reads[g], [("ps", banks[g])])
            for g in range(ng):
                for _ in slots[g]:
                    w_issue()
            for g in range(ng):
                fb = g
                qh = sq_next()
                S.op("act", _act(sq[:, qh, :], hbuf[:, fb, :], AF.Square), reads=[("hb", fb)], writes=[("sq", qh)])
                f = ft_next()
                S.op("act", _act(ft[:, f, :], ps[banks[g]][:, :], AF.Relu), reads=[("ps", banks[g])], writes=[("ft", f)])
                S.op("dve", _tt(hid[:, fb, :], ft[:, f, :], ft[:, f, :], ALU.mult), reads=[("ft", f)], writes=[("hid", fb)])
                stat(SB, onesD, ("onesD",), qh, fb == 0, fb == DC - 1)
                if fb == DC - 1:
                    S.op("act", _act(qe[:, :], ps[SB][:, :], AF.Square, bias=epsT[:, 0:1]),
                         reads=[("ps", SB), ("eps",)], writes=[("qe",)])

        def s34(j):
            nxt = j + 1 < NT
            for half in range(2):
                fb0 = 0
                if half == 0:
                    fb0 = NG3
                    s3_first(NG3)
                for fb in range(fb0, FH):
                    qh = None
                    if half == 0 and fb < DC:
                        qh = sq_next()
                        S.op("act", _act(sq[:, qh, :], hbuf[:, fb, :], AF.Square), reads=[("hb", fb)], writes=[("sq", qh)])
                    bk = bank()
                    wgroup(bk, DC, lambda kk: ab[:, kk, :], lambda kk: ("ab", kk))
                    if qh is not None:
                        stat(SB, onesD, ("onesD",), qh, fb == 0, fb == DC - 1)
                        if fb == DC - 1:
                            S.op("act", _act(qe[:, :], ps[SB][:, :], AF.Square, bias=epsT[:, 0:1]),
                                 reads=[("ps", SB), ("eps",)], writes=[("qe",)])
                    f = ft_next()
                    S.op("act", _act(ft[:, f, :], ps[bk][:, :], AF.Relu), reads=[("ps", bk)], writes=[("ft", f)])
                    S.op("dve", _tt(hid[:, fb, :], ft[:, f, :], ft[:, f, :], ALU.mult), reads=[("ft", f)], writes=[("hid", fb)])
                pend = None
                for n in range(DC):
                    qx = None
                    if nxt and half == 0:
                        qx = prep_a_chunk(j + 1, n)
                    if nxt and half == 1:
                        prep_b_chunk(j + 1, n)
                    bk = bank()
                    wgroup(bk, FH, lambda kk: hid[:, kk, :], lambda kk: ("hid", kk))
                    if qx is not None:
                        stat(SA, onesD, ("onesD",), qx, n == 0, n == DC - 1)
                    if pend is not None:
                        pend()
                        pend = None
                    if half == 0:
                        S.op("act", _act(zbuf[:, n, :], ps[bk][:, :], AF.Identity), reads=[("ps", bk)], writes=[("zb", n)])
                    else:
                        S.op("dve", _tt(zbuf[:, n, :], ps[bk][:, :], zbuf[:, n, :], ALU.add),
                             reads=[("ps", bk), ("zb", n)], writes=[("zb", n)])
                        q = sq_next()
                        S.op("act", _act(sq[:, q, :], zbuf[:, n, :], AF.Square), reads=[("zb", n)], writes=[("sq", q)])
                        pend = (lambda q=q, n=n: stat(SB, onesD, ("onesD",), q, n == 0, n == DC - 1))
                if pend is not None:
                    pend()
                if half == 0 and nxt:
                    prep_a_fin()

        def epilogue_head(j):
            f = ft_next()
            S.op("dve", _stt(ft[:, f, :], qe[:, :], EPS, ps[SB][:, :], ALU.mult, ALU.add),
                 reads=[("qe",), ("ps", SB)], writes=[("ft", f)])
            f2 = ft_next()
            S.op("act", _act(ft[:, f2, :], ft[:, f, :], AF.Sqrt), reads=[("ft", f)], writes=[("ft", f2)])
            S.op("dve", _recip(qe[:, :], ft[:, f2, :]), reads=[("ft", f2)], writes=[("qe",)])

        def epilogue_chunks(j, engines):
            def mk(c):
                def fn():
                    e = engines[c % len(engines)]
                    S.op(e, _stt(zbuf[:, c, :], zbuf[:, c, :], pcol(cfg.o_g4 + c), qe[:, :], ALU.mult, ALU.mult),
                         reads=[("zb", c), ("qe",), ("P",)], writes=[("zb", c)])
                    S.op(e, _tt(zbuf[:, c, :], zbuf[:, c, :], hbuf[:, c, :], ALU.add),
                         reads=[("zb", c), ("hb", c)], writes=[("zb", c)])
                    if c % 4 == 3:
                        g = c // 4
                        S.dma("sp", _dma(outT_v[:, 4 * g:4 * g + 4, j * T:(j + 1) * T], zbuf[:, 4 * g:4 * g + 4, :]),
                              f"st{g}", reads=[("zb", cc) for cc in range(4 * g, 4 * g + 4)])
                        if j + 1 < NT:
                            ld_h(j + 1, g)
                return fn
            return [mk(c) for c in range(DC)]

        build_ident(S, es, nc, ident, smat)

        for g in range(DC // 4):
            ld_h(0, g)
        for c in range(DC):
            q = sq_next()
            S.op("act", _act(sq[:, q, :], hbuf[:, c, :], AF.Square), reads=[("hb", c)], writes=[("sq", q)])
            stat(SA, onesD, ("onesD",), q, c == 0, c == DC - 1)
        prep_a_fin()
        for c in range(DC):
            S.op("dve", _stt(ab[:, c, :], hbuf[:, c, :], pcol(cfg.o_g1 + c), rX[:, :], ALU.mult, ALU.mult),
                 reads=[("hb", c), ("rX",), ("P",)], writes=[("ab", c)])
        epi = []
        for j in range(NT):
            s1(j, epi)
            s2(j)
            s34(j)
            epilogue_head(j)
            epi = epilogue_chunks(j, ["dve"])
        while epi:
            epi.pop(0)()

        def replay(stream, h, final_waits=()):
            waited = {}
            for waits, fn, inc, dsem in stream.ops:
                mx = {}
                for k, v in waits:
                    if mx.get(k, 0) < v:
                        mx[k] = v
                for k, v in mx.items():
                    if waited.get(k, 0) >= v:
                        continue
                    h.wait_ge(sems[k], v)
                    waited[k] = v
                inst = fn(h)
                if inc:
                    inst.then_inc(sems[stream.name], 1)
                elif dsem is not None:
                    inst.then_inc(sems[dsem], 16)
            for k, v in final_waits:
                h.wait_ge(sems[k], v)

        @block.tensor
        def _(h):
            replay(S.E["pe"], h)

        @block.scalar
        def _(h):
            replay(S.E["act"], h)

        @block.vector
        def _(h):
            replay(S.E["dve"], h)

        @block.gpsimd
        def _(h):
            replay(S.E["pool"], h)

        @block.sync
        def _(h):
            fin = [(f"st{g}", 16 * S.dcount[f"st{g}"]) for g in range(DC // 4)]
            replay(S.E["sp"], h, fin)

    return nc


def build_ident(S, es, nc, ident, smat):
    I32 = mybir.dt.int32
    io_f = es.enter_context(nc.sbuf_tensor("io_f", [128, 128], I32))
    io_p = es.enter_context(nc.sbuf_tensor("io_p", [128, 1], I32))
    io_pf = es.enter_context(nc.sbuf_tensor("io_pf", [128, 1], F32))
    S.op("pool", lambda h: h.iota(io_f[:, :], [[1, 128]], base=0, channel_multiplier=0), writes=[("io_f",)])
    S.op("pool", lambda h: h.iota(io_p[:, :], [[1, 1]], base=0, channel_multiplier=1), writes=[("io_p",)])
    S.op("dve", _copy(io_pf[:, :], io_p[:, :]), reads=[("io_p",)], writes=[("io_pf",)])
    S.op("dve", _ts(ident[:, :], io_f[:, :], io_pf[:, 0:1], None, ALU.is_equal),
         reads=[("io_f",), ("io_pf",)], writes=[("ident",)])
    for q in range(4):
        S.op("dve", _copy(smat[32 * q:32 * q + 32, :], ident[32 * q:32 * q + 32, 32 * q:32 * q + 32]),
             reads=[("ident",)], writes=[("smat",)])


_FULL = Cfg()


def kernel(x, mix_pre_gain, w_in, conv_a_w, conv_b_w, conv_b_bias, ln_b_gain, ln_b_bias, w_out,
           mix_post_gain, mlp_pre_gain, w_up, w_down, mlp_post_gain):
    cfg = _FULL
    f = lambda a: np.asarray(a, dtype=np.float32)
    x = f(x)
    nb = x.shape[0]
    prm = pack_params(cfg, f(mix_pre_gain)[0], f(mix_post_gain)[0], f(mlp_pre_gain)[0], f(mlp_post_gain)[0],
                      f(conv_a_w)[0], f(conv_b_w)[0], f(conv_b_bias)[0], f(ln_b_gain)[0], f(ln_b_bias)[0])
    wall = pack_weights(cfg, f(w_in)[0], f(w_out)[0], f(w_up)[0], f(w_down)[0])
    nc = build_program(cfg)
    in_maps = [{"xT": np.ascontiguousarray(x[b].T), "wall": wall, "prm": prm} for b in range(nb)]
    res = run_bass_kernel_spmd(nc, in_maps, core_ids=list(range(nb)))
    out = np.empty_like(x)
    for b in range(nb):
        out[b] = res.results[b]["outT"].T
    return out
```

```python
from contextlib import ExitStack

import numpy as np
import concourse.bass as bass
import concourse.mybir as mybir
from concourse.bass_utils import run_bass_kernel_spmd

F32 = mybir.dt.float32
BF16 = mybir.dt.bfloat16
ALU = mybir.AluOpType
AF = mybir.ActivationFunctionType


class Cfg:
    def __init__(self, D=2048, DA=1024, DB=1024, DFF=8192, S=2048, T=512, KS=16, KA=3, KB=31,
                 NSLOT=8, NXS=3, NSQ=6, NF=6, EPS=1e-6, CONV_TILED=True):
        self.CONV_TILED = CONV_TILED
        self.D, self.DA, self.DB, self.DFF, self.S, self.T = D, DA, DB, DFF, S, T
        self.KS, self.KA, self.KB, self.NSLOT, self.NXS, self.NSQ, self.NF, self.EPS = KS, KA, KB, NSLOT, NXS, NSQ, NF, EPS
        self.DC, self.AC, self.BC, self.FC = D // 128, DA // 128, DB // 128, DFF // 128
        self.MC = self.AC + self.BC
        self.DIN = 3 * DA + 2 * DB
        self.NT = S // T
        self.FH = self.FC // 2
        self.HB = KB - 1
        assert self.DC % KS == 0 and self.MC % KS == 0 and self.FH % KS == 0
        assert self.FH >= self.MC + 2 * self.BC and self.DC % 4 == 0
        self.NU = self.DIN // 128 * (self.DC // KS) + self.DC * (self.MC // KS) \
            + 2 * (self.FH * (self.DC // KS) + self.DC * (self.FH // KS))
        DC, AC, BC = self.DC, self.AC, self.BC
        self.o_g1, self.o_g2, self.o_g3, self.o_g4 = 0, DC, 2 * DC, 3 * DC
        self.o_wa = 4 * DC
        self.o_wb = self.o_wa + AC * KA
        self.o_bb = self.o_wb + BC * KB
        self.o_lg = self.o_bb + BC
        self.o_lb = self.o_lg + BC
        self.NP = self.o_lb + BC


def _chunkvec(v):
    return np.ascontiguousarray(v.reshape(-1, 128).T)


def pack_params(cfg, g1, g2, g3, g4, wa, wb, bb, lg, lb):
    cols = [_chunkvec(g1), _chunkvec(g2), _chunkvec(g3), _chunkvec(g4),
            wa.T.reshape(cfg.AC, 128, cfg.KA).transpose(1, 0, 2).reshape(128, -1),
            wb.T.reshape(cfg.BC, 128, cfg.KB).transpose(1, 0, 2).reshape(128, -1),
            _chunkvec(bb), _chunkvec(lg), _chunkvec(lb)]
    p = np.ascontiguousarray(np.concatenate(cols, axis=1).astype(np.float32))
    assert p.shape == (128, cfg.NP)
    return p


def pack_weights(cfg, w_in, w_out, w_up, w_down):
    KS = cfg.KS
    wall = np.empty((cfg.NU, 128, KS * 128), np.float32)
    u = 0

    def put(W, k0, nb):
        nonlocal u
        blk = W[k0 * 128:(k0 + KS) * 128, nb * 128:(nb + 1) * 128]
        wall[u] = blk.reshape(KS, 128, 128).transpose(1, 0, 2).reshape(128, KS * 128)
        u += 1

    DA, DB = cfg.DA, cfg.DB
    for m in s1_colblocks(cfg):
        for s in range(cfg.DC // KS):
            put(w_in, s * KS, m)
    for n in range(cfg.DC):
        for s in range(cfg.MC // KS):
            put(w_out, s * KS, n)
    for half in range(2):
        for fb in range(cfg.FH):
            for s in range(cfg.DC // KS):
                put(w_up, s * KS, half * cfg.FH + fb)
        for n in range(cfg.DC):
            for s in range(cfg.FH // KS):
                put(w_down, half * cfg.FH + s * KS, n)
    assert u == cfg.NU
    return wall


def s1_colblocks(cfg):
    AC, BC = cfg.AC, cfg.BC
    out = []
    for c in range(BC):
        out += [3 * AC + c, 3 * AC + BC + c]
    for c in range(AC):
        out += [AC + c, 2 * AC + c, c]
    return out


class EngStream:
    def __init__(self, name):
        self.name, self.ops, self.count = name, [], 0

    def add(self, fn, waits, inc=True):
        if inc:
            self.count += 1
        self.ops.append((tuple(waits), fn, inc, None))
        return (self.name, self.count) if inc else None

    def add_dma(self, fn, waits, sem):
        self.ops.append((tuple(waits), fn, False, sem))


class Sched:
    def __init__(self):
        self.E = {n: EngStream(n) for n in ("pe", "act", "dve", "pool", "sp")}
        self.track = {}
        self.const = set()
        self.dcount = {}

    def deps(self, reads, writes):
        w = []
        for k in reads:
            st = self.track.get(k)
            if st and st[0] is not None:
                w.append(st[0])
        for k in writes:
            st = self.track.get(k)
            if st:
                if st[0] is not None:
                    w.append(st[0])
                w.extend(st[1].items())
        return w

    def commit(self, tok, reads, writes):
        for k in reads:
            if k in self.const:
                continue
            st = self.track.setdefault(k, [None, {}])
            if st[1].get(tok[0], 0) < tok[1]:
                st[1][tok[0]] = tok[1]
        for k in writes:
            self.track[k] = [tok, {}]

    def op(self, eng, fn, reads=(), writes=()):
        tok = self.E[eng].add(fn, self.deps(reads, writes), True)
        self.commit(tok, reads, writes)
        return tok

    def dma(self, eng, fn, sem, reads=(), writes=()):
        n = self.dcount.get(sem, 0) + 1
        self.dcount[sem] = n
        self.E[eng].add_dma(fn, self.deps(reads, writes), sem)
        tok = (sem, 16 * n)
        self.commit(tok, reads, writes)
        return tok

    def mm_group(self, out_ap, bank_key, items):
        pe = self.E["pe"]
        n = len(items)
        allreads = []
        tok = None
        for i, (lhsT, rhs, reads) in enumerate(items):
            waits = self.deps(reads, [bank_key] if i == 0 else [])
            last = i == n - 1
            tok = pe.add(_mm(out_ap, lhsT, rhs, i == 0, last), waits, inc=last)
            allreads += list(reads)
        self.commit(tok, allreads, [bank_key])
        return tok

    def stat_mm(self, out_ap, bank_key, lhsT, rhs, first, last, reads):
        waits = self.deps(reads, [bank_key] if first else [])
        tok = self.E["pe"].add(_mm(out_ap, lhsT, rhs, first, last), waits, inc=True)
        self.commit(tok, reads, [bank_key])
        return tok


def _mm(out_ap, lhsT, rhs, start, stop):
    return lambda h: h.matmul(out_ap, lhsT, rhs, start=start, stop=stop)


def _mmt(out_ap, lhsT, rhs, start, stop, tp):
    return lambda h: h.matmul(out_ap, lhsT, rhs, start=start, stop=stop, tile_position=tp)


def _act(out, in_, func, bias=None, scale=None):
    kw = {}
    if bias is not None:
        kw["bias"] = bias
    if scale is not None:
        kw["scale"] = scale
    return lambda h: h.activation(out=out, in_=in_, func=func, **kw)


def _tt(out, in0, in1, op):
    return lambda h: h.tensor_tensor(out=out, in0=in0, in1=in1, op=op)


def _stt(out, in0, scalar, in1, op0, op1):
    return lambda h: h.scalar_tensor_tensor(out=out, in0=in0, scalar=scalar, in1=in1, op0=op0, op1=op1)


def _ts(out, in0, s1, s2, op0, op1=None):
    if op1 is None:
        return lambda h: h.tensor_scalar(out=out, in0=in0, scalar1=s1, scalar2=None, op0=op0)
    return lambda h: h.tensor_scalar(out=out, in0=in0, scalar1=s1, scalar2=s2, op0=op0, op1=op1)


def _copy(out, in_):
    return lambda h: h.tensor_copy(out, in_)


def _recip(out, in_):
    return lambda h: h.reciprocal(out=out, in_=in_)


def _memset(ap, v):
    return lambda h: h.memset(ap, v)


def _dma(out, in_):
    return lambda h: h.dma_start(out=out, in_=in_)


def build_program(cfg):
    D, S_, T = cfg.D, cfg.S, cfg.T
    DC, AC, BC, MC, FH, KS, KA, KB, HB = cfg.DC, cfg.AC, cfg.BC, cfg.MC, cfg.FH, cfg.KS, cfg.KA, cfg.KB, cfg.HB
    NSLOT, NXS, NSQ, NF, EPS, NT, NP, NU = cfg.NSLOT, cfg.NXS, cfg.NSQ, cfg.NF, cfg.EPS, cfg.NT, cfg.NP, cfg.NU

    nc = bass.Bass("TRN2", target_bir_lowering=False)
    xT = nc.dram_tensor("xT", [D, S_], F32, kind="ExternalInput")
    wall = nc.dram_tensor("wall", [NU, 128, KS * 128], F32, kind="ExternalInput")
    prm = nc.dram_tensor("prm", [128, NP], F32, kind="ExternalInput")
    outT = nc.dram_tensor("outT", [D, S_], F32, kind="ExternalOutput")
    outT_v = outT.rearrange("(c p) t -> p c t", p=128)

    with ExitStack() as es:
        def sb(name, shape, dt):
            return es.enter_context(nc.sbuf_tensor(name, shape, dt))

        P = sb("P", [128, NP], F32)
        onesD = sb("onesD", [128, 128], BF16)
        onesB = sb("onesB", [128, 128], BF16)
        ident = sb("ident", [128, 128], BF16)
        epsT = sb("epsT", [128, 1], F32)
        dummy = sb("scr2", [128, 2], F32)
        smat = sb("smat", [128, 32], BF16)
        hbuf = sb("hbuf", [128, DC, T], F32)
        zbuf = sb("zbuf", [128, DC, T], F32)
        ab = sb("ab", [128, DC, T], BF16)
        hid = sb("hid", [128, FH, T], BF16)
        gb = sb("gb", [128, BC, HB + T], BF16)
        dg = sb("dg", [128, 2, KB, 128], BF16)
        xs = sb("xs", [128, NXS, T], F32)
        sq = sb("sq", [128, NSQ, T], BF16)
        wsl = sb("wsl", [128, NSLOT, KS * 128], BF16)
        cvw = sb("cvw", [128, 2, T + KA - 1], F32)
        cvh = sb("cvh", [128, AC, KA - 1], F32)
        ft = sb("ft", [128, NF, T], F32)
        rX = sb("rX", [128, T], F32)
        nmr = sb("nmr", [128, T], F32)
        qe = sb("qe", [128, T], F32)
        ps = [es.enter_context(nc.psum_tensor(f"ps{b}", [128, T], F32)) for b in range(8)]
        hidF = hid.bitcast(F32)

        def G(c):
            return bass.AP(hidF, MC * (T // 2) + c * T, [[FH * (T // 2), 128], [1, T]])

        def Gkeys(c):
            return [("hid", MC + 2 * c), ("hid", MC + 2 * c + 1)]

        def pcol(off):
            return P[:, off:off + 1]

        semnames = ["pe", "act", "dve", "pool", "sp", "prm"] + [f"w{i}" for i in range(NSLOT)] \
            + [f"xs{i}" for i in range(NXS)] + [f"st{i}" for i in range(DC // 4)] + [f"hx{i}" for i in range(DC // 4)]
        sems = {n: es.enter_context(nc.semaphore(n)) for n in semnames}
        block = es.enter_context(nc.Block())

        S = Sched()
        S.const |= {("P",), ("onesD",), ("onesB",), ("ident",), ("eps",), ("smat",)}
        SA, SB = 6, 7
        state = {"bank": 0, "xs": 0, "sq": 0, "ft": 0, "wl": 0, "wu": 0}
        sq_pending = [False] * NSQ

        def bank():
            b = state["bank"]
            state["bank"] = (b + 1) % 6
            return b

        def xs_next():
            k = state["xs"]
            state["xs"] = (k + 1) % NXS
            return k

        def sq_next():
            q = state["sq"]
            state["sq"] = (q + 1) % NSQ
            assert not sq_pending[q], "sq slot reused before its consumer was emitted"
            sq_pending[q] = True
            return q

        def ft_next():
            f = state["ft"]
            state["ft"] = (f + 1) % NF
            return f

        def stat(bk, ones, ones_key, q, first, last):
            S.stat_mm(ps[bk][:, :], ("ps", bk), ones[:, :], sq[:, q, :], first, last, [("sq", q), ones_key])
            sq_pending[q] = False

        class StatBatch:
            def __init__(self, bk, ones, ones_key, n_total, bs=4):
                self.bk, self.ones, self.ones_key, self.n_total, self.bs = bk, ones, ones_key, n_total, bs
                self.items, self.done = [], 0

            def add(self, q):
                self.items.append(q)

            def maybe(self, force=False):
                if len(self.items) >= self.bs or (force and self.items):
                    for q in self.items:
                        stat(self.bk, self.ones, self.ones_key, q, self.done == 0, self.done == self.n_total - 1)
                        self.done += 1
                    self.items = []

        total_loads = NU * NT

        def w_issue():
            l = state["wl"]
            if l >= total_loads:
                return
            slot, u = l % NSLOT, l % NU
            S.dma("pool", _dma(wsl[:, slot, :], wall[u, :, :]), f"w{slot}", writes=[("w", slot)])
            state["wl"] = l + 1

        def w_take(n):
            l = state["wu"]
            state["wu"] = l + n
            assert state["wu"] <= state["wl"]
            return [(l + i) % NSLOT for i in range(n)]

        def wgroup(bk, nk, rhs_of, rkey_of):
            slots = w_take(nk // KS)
            items = []
            for kk in range(nk):
                s = slots[kk // KS]
                o = (kk % KS) * 128
                items.append((wsl[:, s, o:o + 128], rhs_of(kk), [("w", s), rkey_of(kk)]))
            tok = S.mm_group(ps[bk][:, :], ("ps", bk), items)
            for _ in slots:
                w_issue()
            return tok

        def ld_x(j, c, k):
            S.dma("sp", _dma(xs[:, k, :], xT[c * 128:(c + 1) * 128, j * T:(j + 1) * T]), f"xs{k}",
                  writes=[("xs", k)])

        xT_v = xT.rearrange("(c p) t -> p c t", p=128)
        S.dma("sp", _dma(P[:, :], prm[:, :]), "prm", writes=[("P",)])
        S.op("dve", _memset(onesD[:, :], 1.0 / D), writes=[("onesD",)])
        S.op("dve", _memset(onesB[:, :], 1.0 / cfg.DB), writes=[("onesB",)])
        S.op("dve", _memset(epsT[:, :], EPS), writes=[("eps",)])
        for c in range(BC):
            S.op("dve", _memset(gb[:, c, 0:HB], 0.0), writes=[("gbh", c)])
        for c in range(AC):
            S.op("dve", _memset(cvh[:, c, :], 0.0), writes=[("cvh", c)])

        for _ in range(NSLOT):
            w_issue()

        def ld_h(j, g):
            S.dma("sp", _dma(hbuf[:, 4 * g:4 * g + 4, :], xT_v[:, 4 * g:4 * g + 4, j * T:(j + 1) * T]), f"hx{g}",
                  writes=[("hb", c) for c in range(4 * g, 4 * g + 4)])

        def prep_a_chunk(j, c):
            k = xs_next()
            ld_x(j, c, k)
            q = sq_next()
            S.op("act", _act(sq[:, q, :], xs[:, k, :], AF.Square), reads=[("xs", k)], writes=[("sq", q)])
            return q

        def prep_a_fin():
            f = ft_next()
            S.op("act", _act(ft[:, f, :], ps[SA][:, :], AF.Sqrt, bias=epsT[:, 0:1]),
                 reads=[("ps", SA), ("eps",)], writes=[("ft", f)])
            S.op("dve", _recip(rX[:, :], ft[:, f, :]), reads=[("ft", f)], writes=[("rX",)])

        def prep_b_chunk(j, c):
            k = xs_next()
            ld_x(j, c, k)
            S.op("dve", _stt(ab[:, c, :], xs[:, k, :], pcol(cfg.o_g1 + c), rX[:, :], ALU.mult, ALU.mult),
                 reads=[("xs", k), ("rX",), ("P",)], writes=[("ab", c)])

        def cdiv(a, b):
            return -(-a // b)

        def s1(j, epi):
            pend = []

            def tick():
                for p in pend:
                    p[0] -= 1
                ready = sorted([p for p in pend if p[0] <= 0], key=lambda p: p[1])
                for p in ready:
                    pend.remove(p)
                for p in ready:
                    p[2]()

            def flush():
                while pend:
                    tick()

            def conv_tiled(c):
                b = c % 2
                banks4 = [bank() for _ in range(4)]
                pe = S.E["pe"]
                rd = [("dg", b), ("gbc", c), ("gbh", c)]
                wr = [("ps", bk) for bk in banks4]
                n_total, cnt, tok = KB * 4, 0, None
                for r in range(cdiv(KB, 4)):
                    for jq in range(4):
                        k = 4 * r + jq
                        if k >= KB:
                            continue
                        for i in range(4):
                            cnt += 1
                            waits = S.deps(rd, wr) if cnt == 1 else []
                            fin = cnt == n_total
                            t = pe.add(_mmt(ps[banks4[i]][32 * jq:32 * jq + 32, :],
                                            dg[32 * i:32 * i + 32, b, k, 32 * i:32 * i + 32],
                                            gb[32 * i:32 * i + 32, c, k:k + T],
                                            r == 0, k + 4 >= KB, (32 * i, 32 * jq)), waits, inc=fin)
                            if fin:
                                tok = t
                S.commit(tok, rd, wr)
                qs = []
                for i in range(4):
                    q = sq_next()
                    S.op("act", _act(sq[:, q, :], ps[banks4[i]][:, :], AF.Identity), reads=[("ps", banks4[i])], writes=[("sq", q)])
                    qs.append(q)
                return qs

            def conv_reduce(c, qs):
                pe = S.E["pe"]
                bR = bank()
                rtok = None
                for i in range(4):
                    rdq = [("sq", qs[i]), ("smat",)]
                    waits = S.deps(rdq, [("ps", bR)] if i == 0 else [])
                    rtok = pe.add(_mmt(ps[bR][32 * i:32 * i + 32, :], smat[:, :], sq[:, qs[i], :], True, True, (0, 32 * i)),
                                  waits, inc=(i == 3))
                S.commit(rtok, [("sq", q) for q in qs], [("ps", bR)])
                for q in qs:
                    sq_pending[q] = False
                return bR

            def conv(c):
                b = c % 2
                if cfg.CONV_TILED:
                    qs = conv_tiled(c)
                    pend.append([1, 1, (lambda c=c, qs=qs: conv_post(c, conv_reduce(c, qs)))])
                    return
                else:
                    bk = bank()
                    items = [(dg[:, b, k, :], gb[:, c, k:k + T], [("dg", b), ("gbc", c), ("gbh", c)]) for k in range(KB)]
                    S.mm_group(ps[bk][:, :], ("ps", bk), items)
                conv_post(c, bk)

            def conv_post(c, bk):
                S.op("dve", _copy(gb[:, c, 0:HB], gb[:, c, T:T + HB]), reads=[("gbc", c)], writes=[("gbh", c)])
                S.op("act", _act(G(c), ps[bk][:, :], AF.Identity, bias=pcol(cfg.o_bb + c)),
                     reads=[("ps", bk), ("P",)], writes=Gkeys(c))
                q1 = sq_next()
                S.op("act", _act(sq[:, q1, :], G(c), AF.Identity), reads=Gkeys(c), writes=[("sq", q1)])
                q2 = sq_next()
                S.op("act", _act(sq[:, q2, :], G(c), AF.Square), reads=Gkeys(c), writes=[("sq", q2)])

                def st():
                    stat(SA, onesB, ("onesB",), q1, c == 0, c == BC - 1)
                    stat(SB, onesB, ("onesB",), q2, c == 0, c == BC - 1)
                pend.append([1, 0, st])

            def lnfin():
                fm = ft_next()
                S.op("act", _act(ft[:, fm, :], ps[SA][:, :], AF.Identity), reads=[("ps", SA)], writes=[("ft", fm)])
                f2 = ft_next()
                S.op("dve", _tt(ft[:, f2, :], ft[:, fm, :], ft[:, fm, :], ALU.mult), reads=[("ft", fm)], writes=[("ft", f2)])
                f3 = ft_next()
                S.op("dve", _tt(ft[:, f3, :], ps[SB][:, :], ft[:, f2, :], ALU.subtract),
                     reads=[("ps", SB), ("ft", f2)], writes=[("ft", f3)])
                f4 = ft_next()
                S.op("act", _act(ft[:, f4, :], ft[:, f3, :], AF.Sqrt, bias=epsT[:, 0:1]),
                     reads=[("ft", f3), ("eps",)], writes=[("ft", f4)])
                S.op("dve", _recip(rX[:, :], ft[:, f4, :]), reads=[("ft", f4)], writes=[("rX",)])
                S.op("dve", _stt(nmr[:, :], ft[:, fm, :], -1.0, rX[:, :], ALU.mult, ALU.mult),
                     reads=[("ft", fm), ("rX",)], writes=[("nmr",)])

            def lnapply(c):
                f = ft_next()
                S.op("dve", _tt(ft[:, f, :], G(c), rX[:, :], ALU.mult), reads=Gkeys(c) + [("rX",)], writes=[("ft", f)])
                S.op("dve", _tt(ft[:, f, :], ft[:, f, :], nmr[:, :], ALU.add), reads=[("ft", f), ("nmr",)], writes=[("ft", f)])
                S.op("act", _act(hid[:, AC + c, :], ft[:, f, :], AF.Silu, bias=pcol(cfg.o_lb + c), scale=pcol(cfg.o_lg + c)),
                     reads=[("ft", f), ("P",)], writes=[("hid", AC + c)])

            for c in range(BC):
                bv, bg = bank(), bank()
                wgroup(bv, DC, lambda kk: ab[:, kk, :], lambda kk: ("ab", kk))
                wgroup(bg, DC, lambda kk: ab[:, kk, :], lambda kk: ("ab", kk))
                f = ft_next()
                S.op("act", _act(ft[:, f, :], ps[bg][:, :], AF.Sigmoid), reads=[("ps", bg)], writes=[("ft", f)])
                S.op("dve", _tt(gb[:, c, HB:HB + T], ps[bv][:, :], ft[:, f, :], ALU.mult),
                     reads=[("ps", bv), ("ft", f)], writes=[("gbc", c)])
                b = c % 2
                wbc = P[:, cfg.o_wb + c * KB: cfg.o_wb + (c + 1) * KB].unsqueeze(2).broadcast_to([128, KB, 128])
                idc = ident[:, :].unsqueeze(1).broadcast_to([128, KB, 128])
                S.op("dve", _tt(dg[:, b, :, :], idc, wbc, ALU.mult), reads=[("ident",), ("P",)], writes=[("dg", b)])
                pend.append([2, 2, (lambda c=c: conv(c))])
                tick()
                for _ in range(cdiv(len(epi), BC - c)):
                    epi.pop(0)()

            ln = {"fin": False, "next": 0}
            for i in range(AC):
                bC, bV, bB = bank(), bank(), bank()
                wgroup(bC, DC, lambda kk: ab[:, kk, :], lambda kk: ("ab", kk))
                wgroup(bV, DC, lambda kk: ab[:, kk, :], lambda kk: ("ab", kk))
                wgroup(bB, DC, lambda kk: ab[:, kk, :], lambda kk: ("ab", kk))
                c = i
                f1 = ft_next()
                S.op("act", _act(ft[:, f1, :], ps[bC][:, :], AF.Identity), reads=[("ps", bC)], writes=[("ft", f1)])
                w = c % 2
                S.op("dve", _copy(cvw[:, w, 0:KA - 1], cvh[:, c, :]), reads=[("cvh", c)], writes=[("cvw", w)])
                S.op("dve", _tt(cvw[:, w, KA - 1:KA - 1 + T], ps[bV][:, :], ft[:, f1, :], ALU.mult),
                     reads=[("ps", bV), ("ft", f1)], writes=[("cvw", w)])
                f2 = ft_next()
                S.op("dve", _ts(ft[:, f2, :], cvw[:, w, KA - 1:KA - 1 + T], pcol(cfg.o_wa + c * KA + KA - 1), None, ALU.mult),
                     reads=[("cvw", w), ("P",)], writes=[("ft", f2)])
                for k in range(KA - 2, -1, -1):
                    S.op("dve", _stt(ft[:, f2, :], cvw[:, w, k:k + T], pcol(cfg.o_wa + c * KA + k), ft[:, f2, :], ALU.mult, ALU.add),
                         reads=[("cvw", w), ("ft", f2), ("P",)], writes=[("ft", f2)])
                S.op("dve", _tt(hid[:, c, :], ps[bB][:, :], ft[:, f2, :], ALU.mult),
                     reads=[("ps", bB), ("ft", f2)], writes=[("hid", c)])
                S.op("dve", _copy(cvh[:, c, :], cvw[:, w, T:T + KA - 1]), reads=[("cvw", w)], writes=[("cvh", c)])
                tick()
                for _ in range(cdiv(len(epi), AC - i)):
                    epi.pop(0)()
                if not ln["fin"]:
                    if not pend:
                        lnfin()
                        ln["fin"] = True
                else:
                    for _ in range(cdiv(BC - ln["next"], max(1, AC - 2 - i))):
                        if ln["next"] < BC:
                            lnapply(ln["next"])
                            ln["next"] += 1
            flush()
            while epi:
                epi.pop(0)()
            if not ln["fin"]:
                lnfin()
            while ln["next"] < BC:
                lnapply(ln["next"])
                ln["next"] += 1
            S.op("act", _act(dummy[:, 1:2], epsT[:, 0:1], AF.Sqrt), reads=[("eps",)], writes=[("dummy",)])

        def s2(j):
            sb2 = StatBatch(SA, onesD, ("onesD",), DC)
            for n in range(DC):
                bk = bank()
                wgroup(bk, MC, lambda kk: hid[:, kk, :], lambda kk: ("hid", kk))
                sb2.maybe()
                S.op("act", _act(zbuf[:, n, :], ps[bk][:, :], AF.Identity, scale=pcol(cfg.o_g2 + n)),
                     reads=[("ps", bk), ("P",)], writes=[("zb", n)])
                q = sq_next()
                S.op("act", _act(sq[:, q, :], ps[bk][:, :], AF.Square), reads=[("ps", bk)], writes=[("sq", q)])
                sb2.add(q)
            sb2.maybe(force=True)
            f = ft_next()
            S.op("act", _act(ft[:, f, :], ps[SA][:, :], AF.Sqrt, bias=epsT[:, 0:1]),
                 reads=[("ps", SA), ("eps",)], writes=[("ft", f)])
            S.op("dve", _recip(rX[:, :], ft[:, f, :]), reads=[("ft", f)], writes=[("rX",)])
            for c in range(DC):
                e = "dve"
                S.op(e, _tt(zbuf[:, c, :], zbuf[:, c, :], rX[:, :], ALU.mult),
                     reads=[("zb", c), ("rX",)], writes=[("zb", c)])
                S.op(e, _tt(hbuf[:, c, :], zbuf[:, c, :], hbuf[:, c, :], ALU.add),
                     reads=[("zb", c), ("hb", c)], writes=[("hb", c)])
                S.op("act", _act(ab[:, c, :], hbuf[:, c, :], AF.Identity, scale=pcol(cfg.o_g3 + c)),
                     reads=[("hb", c), ("P",)], writes=[("ab", c)])

        NG3 = min(5, FH, DC)

        def s3_first(ng, sbh):
            banks = [bank() for _ in range(ng)]
            slots = [w_take(DC // KS) for _ in range(ng)]
            pe = S.E["pe"]
            toks = [None] * ng
            reads = [[] for _ in range(ng)]
            for kk in range(DC):
                for g in range(ng):
                    sl = slots[g][kk // KS]
                    o = (kk % KS) * 128
                    rd = [("w", sl), ("ab", kk)]
                    bk = banks[g]
                    waits = S.deps(rd, [("ps", bk)] if kk == 0 else [])
                    last = kk == DC - 1
                    t = pe.add(_mm(ps[bk][:, :], wsl[:, sl, o:o + 128], ab[:, kk, :], kk == 0, last), waits, inc=last)
                    reads[g] += rd
                    if last:
                        toks[g] = t
            for g in range(ng):
                S.commit(toks[g], reads[g], [("ps", banks[g])])
            for g in range(ng):
                for _ in slots[g]:
                    w_issue()
            for g in range(ng):
                fb = g
                qh = sq_next()
                S.op("act", _act(sq[:, qh, :], hbuf[:, fb, :], AF.Square), reads=[("hb", fb)], writes=[("sq", qh)])
                f = ft_next()
                S.op("act", _act(ft[:, f, :], ps[banks[g]][:, :], AF.Relu), reads=[("ps", banks[g])], writes=[("ft", f)])
                S.op("dve", _tt(hid[:, fb, :], ft[:, f, :], ft[:, f, :], ALU.mult), reads=[("ft", f)], writes=[("hid", fb)])
                sbh.add(qh)
                sbh.maybe(force=(fb == DC - 1))
                if fb == DC - 1:
                    S.op("act", _act(qe[:, :], ps[SB][:, :], AF.Square, bias=epsT[:, 0:1]),
                         reads=[("ps", SB), ("eps",)], writes=[("qe",)])

        def s34(j):
            nxt = j + 1 < NT
            for half in range(2):
                fb0 = 0
                if half == 0:
                    fb0 = NG3
                    sbh = StatBatch(SB, onesD, ("onesD",), DC)
                    s3_first(NG3, sbh)
                for fb in range(fb0, FH):
                    qh = None
                    if half == 0 and fb < DC:
                        qh = sq_next()
                        S.op("act", _act(sq[:, qh, :], hbuf[:, fb, :], AF.Square), reads=[("hb", fb)], writes=[("sq", qh)])
                        sbh.add(qh)
                    bk = bank()
                    wgroup(bk, DC, lambda kk: ab[:, kk, :], lambda kk: ("ab", kk))
                    if qh is not None:
                        sbh.maybe(force=(fb == DC - 1))
                        if fb == DC - 1:
                            S.op("act", _act(qe[:, :], ps[SB][:, :], AF.Square, bias=epsT[:, 0:1]),
                                 reads=[("ps", SB), ("eps",)], writes=[("qe",)])
                    f = ft_next()
                    S.op("act", _act(ft[:, f, :], ps[bk][:, :], AF.Relu), reads=[("ps", bk)], writes=[("ft", f)])
                    S.op("dve", _tt(hid[:, fb, :], ft[:, f, :], ft[:, f, :], ALU.mult), reads=[("ft", f)], writes=[("hid", fb)])
                sbx = StatBatch(SA, onesD, ("onesD",), DC, bs=2)
                sbz = StatBatch(SB, onesD, ("onesD",), DC)
                for n in range(DC):
                    if nxt and half == 0:
                        sbx.add(prep_a_chunk(j + 1, n))
                    if nxt and half == 1:
                        prep_b_chunk(j + 1, n)
                    bk = bank()
                    wgroup(bk, FH, lambda kk: hid[:, kk, :], lambda kk: ("hid", kk))
                    sbx.maybe(force=(n == DC - 1))
                    sbz.maybe()
                    if half == 0:
                        S.op("act", _act(zbuf[:, n, :], ps[bk][:, :], AF.Identity), reads=[("ps", bk)], writes=[("zb", n)])
                    else:
                        S.op("dve", _tt(zbuf[:, n, :], ps[bk][:, :], zbuf[:, n, :], ALU.add),
                             reads=[("ps", bk), ("zb", n)], writes=[("zb", n)])
                        q = sq_next()
                        S.op("act", _act(sq[:, q, :], zbuf[:, n, :], AF.Square), reads=[("zb", n)], writes=[("sq", q)])
                        sbz.add(q)
                sbz.maybe(force=True)
                if half == 0 and nxt:
                    prep_a_fin()

        def epilogue_head(j):
            f = ft_next()
            S.op("dve", _stt(ft[:, f, :], qe[:, :], EPS, ps[SB][:, :], ALU.mult, ALU.add),
                 reads=[("qe",), ("ps", SB)], writes=[("ft", f)])
            f2 = ft_next()
            S.op("act", _act(ft[:, f2, :], ft[:, f, :], AF.Sqrt), reads=[("ft", f)], writes=[("ft", f2)])
            S.op("dve", _recip(qe[:, :], ft[:, f2, :]), reads=[("ft", f2)], writes=[("qe",)])

        def epilogue_chunks(j, engines):
            def mk(c):
                def fn():
                    e = engines[c % len(engines)]
                    S.op(e, _stt(zbuf[:, c, :], zbuf[:, c, :], pcol(cfg.o_g4 + c), qe[:, :], ALU.mult, ALU.mult),
                         reads=[("zb", c), ("qe",), ("P",)], writes=[("zb", c)])
                    S.op(e, _tt(zbuf[:, c, :], zbuf[:, c, :], hbuf[:, c, :], ALU.add),
                         reads=[("zb", c), ("hb", c)], writes=[("zb", c)])
                    if c % 4 == 3:
                        g = c // 4
                        S.dma("sp", _dma(outT_v[:, 4 * g:4 * g + 4, j * T:(j + 1) * T], zbuf[:, 4 * g:4 * g + 4, :]),
                              f"st{g}", reads=[("zb", cc) for cc in range(4 * g, 4 * g + 4)])
                        if j + 1 < NT:
                            ld_h(j + 1, g)
                return fn
            return [mk(c) for c in range(DC)]

        build_ident(S, es, nc, ident, smat)

        S.op("act", _act(dummy[:, 0:1], epsT[:, 0:1], AF.Sqrt), reads=[("eps",)], writes=[("dummy0",)])
        for g in range(DC // 4):
            ld_h(0, g)
        for c in range(DC):
            q = sq_next()
            S.op("act", _act(sq[:, q, :], hbuf[:, c, :], AF.Square), reads=[("hb", c)], writes=[("sq", q)])
            stat(SA, onesD, ("onesD",), q, c == 0, c == DC - 1)
        prep_a_fin()
        for c in range(DC):
            S.op("dve", _stt(ab[:, c, :], hbuf[:, c, :], pcol(cfg.o_g1 + c), rX[:, :], ALU.mult, ALU.mult),
                 reads=[("hb", c), ("rX",), ("P",)], writes=[("ab", c)])
        epi = []
        for j in range(NT):
            s1(j, epi)
            s2(j)
            s34(j)
            epilogue_head(j)
            epi = epilogue_chunks(j, ["dve"])
        while epi:
            epi.pop(0)()

        def replay(stream, h, final_waits=()):
            waited = {}
            for waits, fn, inc, dsem in stream.ops:
                mx = {}
                for k, v in waits:
                    if mx.get(k, 0) < v:
                        mx[k] = v
                for k, v in mx.items():
                    if waited.get(k, 0) >= v:
                        continue
                    h.wait_ge(sems[k], v)
                    waited[k] = v
                inst = fn(h)
                if inc:
                    inst.then_inc(sems[stream.name], 1)
                elif dsem is not None:
                    inst.then_inc(sems[dsem], 16)
            for k, v in final_waits:
                h.wait_ge(sems[k], v)

        @block.tensor
        def _(h):
            replay(S.E["pe"], h)

        @block.scalar
        def _(h):
            replay(S.E["act"], h)

        @block.vector
        def _(h):
            replay(S.E["dve"], h)

        @block.gpsimd
        def _(h):
            replay(S.E["pool"], h)

        @block.sync
        def _(h):
            fin = [(f"st{g}", 16 * S.dcount[f"st{g}"]) for g in range(DC // 4)]
            replay(S.E["sp"], h, fin)

    return nc


def build_ident(S, es, nc, ident, smat):
    I32 = mybir.dt.int32
    io_f = es.enter_context(nc.sbuf_tensor("io_f", [128, 128], I32))
    io_p = es.enter_context(nc.sbuf_tensor("io_p", [128, 1], I32))
    io_pf = es.enter_context(nc.sbuf_tensor("io_pf", [128, 1], F32))
    S.op("pool", lambda h: h.iota(io_f[:, :], [[1, 128]], base=0, channel_multiplier=0), writes=[("io_f",)])
    S.op("pool", lambda h: h.iota(io_p[:, :], [[1, 1]], base=0, channel_multiplier=1), writes=[("io_p",)])
    S.op("dve", _copy(io_pf[:, :], io_p[:, :]), reads=[("io_p",)], writes=[("io_pf",)])
    S.op("dve", _ts(ident[:, :], io_f[:, :], io_pf[:, 0:1], None, ALU.is_equal),
         reads=[("io_f",), ("io_pf",)], writes=[("ident",)])
    for q in range(4):
        S.op("dve", _copy(smat[32 * q:32 * q + 32, :], ident[32 * q:32 * q + 32, 32 * q:32 * q + 32]),
             reads=[("ident",)], writes=[("smat",)])


_FULL = Cfg()


def kernel(x, mix_pre_gain, w_in, conv_a_w, conv_b_w, conv_b_bias, ln_b_gain, ln_b_bias, w_out,
           mix_post_gain, mlp_pre_gain, w_up, w_down, mlp_post_gain):
    cfg = _FULL
    f = lambda a: np.asarray(a, dtype=np.float32)
    x = f(x)
    nb = x.shape[0]
    prm = pack_params(cfg, f(mix_pre_gain)[0], f(mix_post_gain)[0], f(mlp_pre_gain)[0], f(mlp_post_gain)[0],
                      f(conv_a_w)[0], f(conv_b_w)[0], f(conv_b_bias)[0], f(ln_b_gain)[0], f(ln_b_bias)[0])
    wall = pack_weights(cfg, f(w_in)[0], f(w_out)[0], f(w_up)[0], f(w_down)[0])
    nc = build_program(cfg)
    in_maps = [{"xT": np.ascontiguousarray(x[b].T), "wall": wall, "prm": prm} for b in range(nb)]
    res = run_bass_kernel_spmd(nc, in_maps, core_ids=list(range(nb)))
    out = np.empty_like(x)
    for b in range(nb):
        out[b] = res.results[b]["outT"].T
    return out
```
